# Optimizing a Trainium2 kernel written in Bass

```python
import jax, jax.numpy as jnp
from jax import lax
import numpy as np

D_MODEL = 2048
BATCH = 16
SEQ = 2048
DEPTH = 1
DEC_BATCH = 32
DEC_SEQ = 16
PAST_LEN = 2048

CHUNK = 64
Q_BLOCK = 128
HEAD_DIM = 128
SB_HEADS = 8
RET_HEADS = 8
SB_WIDTH = SB_HEADS * HEAD_DIM
RET_WIDTH = RET_HEADS * HEAD_DIM
MIX_WIDTH = SB_WIDTH + RET_WIDTH
IN_WIDTH = 4 * SB_WIDTH + 4 * RET_WIDTH
ROPE_BASE = 10000.0
EPS = 1e-6

kernel_name = "sandwich_hymba_stickbreak_retention_step"


def rms_norm(x, g):
    xf = x.astype(jnp.float32)
    y = xf * lax.rsqrt(jnp.mean(xf * xf, axis=-1, keepdims=True) + EPS)
    return (y * g.astype(jnp.float32)).astype(x.dtype)


def head_rms_norm(o, g, dtype):
    b, t, h, d = o.shape
    of = o.astype(jnp.float32)
    y = of * lax.rsqrt(jnp.mean(of * of, axis=-1, keepdims=True) + EPS)
    y = y.reshape(b, t, h * d) * g.astype(jnp.float32)
    return y.astype(dtype)


def rope(x, pos):
    half = HEAD_DIM // 2
    inv = ROPE_BASE ** (-jnp.arange(half, dtype=jnp.float32) / half)
    ang = pos.astype(jnp.float32)[:, None] * inv[None, :]
    cos = jnp.cos(ang)[None, :, None, :]
    sin = jnp.sin(ang)[None, :, None, :]
    xf = x.astype(jnp.float32)
    x1, x2 = xf[..., :half], xf[..., half:]
    out = jnp.concatenate([x1 * cos - x2 * sin, x1 * sin + x2 * cos], axis=-1)
    return out.astype(x.dtype)


def project(h, w_in, pos):
    b, t, _ = h.shape
    p = jnp.einsum('btd,de->bte', h, w_in)
    widths = [SB_WIDTH] * 4 + [RET_WIDTH] * 4
    splits = list(np.cumsum(widths)[:-1])
    q_sb, k_sb, v_sb, gate_sb, q_r, k_r, v_r, gate_r = jnp.split(p, splits, axis=-1)
    heads = lambda a, nh: a.reshape(b, t, nh, HEAD_DIM)
    q_sb, k_sb, v_sb = heads(q_sb, SB_HEADS), heads(k_sb, SB_HEADS), heads(v_sb, SB_HEADS)
    q_r = rope(heads(q_r, RET_HEADS), pos)
    k_r = rope(heads(k_r, RET_HEADS), pos) * (HEAD_DIM ** -0.5)
    v_r = heads(v_r, RET_HEADS)
    return q_sb, k_sb, v_sb, gate_sb, q_r, k_r, v_r, gate_r


def stick_breaking_block(q, k, v, q_start):
    tq, tk = q.shape[1], k.shape[1]
    z = jnp.einsum('bqhd,bkhd->bhqk', q.astype(jnp.float32), k.astype(jnp.float32)) * (HEAD_DIM ** -0.5)
    q_pos = q_start + jnp.arange(tq)
    k_pos = jnp.arange(tk)
    strict = (k_pos[None, :] < q_pos[:, None])[None, None]
    log_beta = jax.nn.log_sigmoid(z)
    log_keep = jnp.where(strict, jax.nn.log_sigmoid(-z), 0.0)
    later = lax.cumsum(log_keep, axis=3, reverse=True) - log_keep
    a = jnp.where(strict, jnp.exp(log_beta + later), 0.0)
    return jnp.einsum('bhqk,bkhd->bqhd', a.astype(v.dtype), v)


def retention_chunk(q, k, v, s, log_gamma):
    n = q.shape[1]
    idx = jnp.arange(n, dtype=jnp.float32)
    lg = log_gamma[:, None]
    decay_read = jnp.exp(lg * (idx + 1.0))
    decay_write = jnp.exp(lg * (jnp.float32(n) - 1.0 - idx))
    intra = jnp.exp(lg[:, :, None] * jnp.abs(idx[:, None] - idx[None, :]))
    qf, kf, vf = q.astype(jnp.float32), k.astype(jnp.float32), v.astype(jnp.float32)
    sf = s.astype(jnp.float32)
    scores = jnp.einsum('bihd,bjhd->bhij', qf, kf) * intra[None]
    o = jnp.einsum('bhij,bjhe->bihe', scores, vf)
    o = o + jnp.einsum('bihd,bhde->bihe', qf, sf) * decay_read.T[None, :, :, None]
    s_new = sf * jnp.exp(lg * jnp.float32(n))[None, :, :, None] + jnp.einsum('bjhd,hj,bjhe->bhde', kf, decay_write, vf)
    return o.astype(v.dtype), s_new.astype(s.dtype)


def retention_prompt(q, k, v, s0, log_gamma):
    b, t, h, d = q.shape
    nc = t // CHUNK
    to_chunks = lambda a: a.reshape(b, nc, CHUNK, h, d).transpose(1, 0, 2, 3, 4)

    def step(s, qkv):
        qc, kc, vc = qkv
        o, s = retention_chunk(qc, kc, vc, s, log_gamma)
        return s, o

    s_fin, o = lax.scan(step, s0, (to_chunks(q), to_chunks(k), to_chunks(v)))
    return o.transpose(1, 0, 2, 3, 4).reshape(b, t, h, d), s_fin


def merge(o_sb, o_r, gate_sb, gate_r, g_sb, g_r, w_out):
    y_sb = head_rms_norm(o_sb, g_sb, gate_sb.dtype) * jax.nn.silu(gate_sb)
    y_r = head_rms_norm(o_r, g_r, gate_r.dtype) * jax.nn.silu(gate_r)
    y = jnp.concatenate([y_sb, y_r], axis=-1)
    return jnp.einsum('bte,ed->btd', y, w_out)


def setup_inputs(seed: int = 0) -> dict:
    key = jax.random.key(seed)
    ks = jax.random.split(key, 11)
    f32 = jnp.float32
    nrm = jax.random.normal
    return {
        "x_prompt": nrm(ks[0], (BATCH, SEQ, D_MODEL), f32),
        "x_sample": nrm(ks[1], (DEC_BATCH, DEC_SEQ, D_MODEL), f32),
        "cache_sb_k": nrm(ks[2], (DEPTH, DEC_BATCH, PAST_LEN, SB_HEADS, HEAD_DIM), f32),
        "cache_sb_v": nrm(ks[3], (DEPTH, DEC_BATCH, PAST_LEN, SB_HEADS, HEAD_DIM), f32),
        "state_ret": nrm(ks[4], (DEPTH, DEC_BATCH, RET_HEADS, HEAD_DIM, HEAD_DIM), f32),
        "norm_pre": 1.0 + 0.05 * nrm(ks[5], (DEPTH, D_MODEL), f32),
        "w_in": nrm(ks[6], (DEPTH, D_MODEL, IN_WIDTH), f32) * (D_MODEL ** -0.5),
        "sb_head_norm": 1.0 + 0.05 * nrm(ks[7], (DEPTH, SB_WIDTH), f32),
        "ret_head_norm": 1.0 + 0.05 * nrm(ks[8], (DEPTH, RET_WIDTH), f32),
        "w_out": nrm(ks[9], (DEPTH, MIX_WIDTH, D_MODEL), f32) * (MIX_WIDTH ** -0.5),
        "norm_post": 1.0 + 0.05 * nrm(ks[10], (DEPTH, D_MODEL), f32),
    }


def reference(x_prompt, x_sample, cache_sb_k, cache_sb_v, state_ret,
              norm_pre, w_in, sb_head_norm, ret_head_norm, w_out, norm_post):
    log_gamma = jnp.log(1.0 - 2.0 ** (-5.0 - jnp.arange(RET_HEADS, dtype=jnp.float32)))
    t_p = x_prompt.shape[1]
    t_s = x_sample.shape[1]
    past = cache_sb_k.shape[2]
    pos_p = jnp.arange(t_p, dtype=jnp.int32)
    pos_s = past + jnp.arange(t_s, dtype=jnp.int32)
    xp, xs = x_prompt, x_sample
    kp_l, vp_l, sp_l, ks_l, vs_l, ss_l = [], [], [], [], [], []
    for l in range(DEPTH):
        h = rms_norm(xp, norm_pre[l])
        q_sb, k_sb, v_sb, gate_sb, q_r, k_r, v_r, gate_r = project(h, w_in[l], pos_p)
        blocks = []
        for i0 in range(0, t_p, Q_BLOCK):
            i1 = min(i0 + Q_BLOCK, t_p)
            blocks.append(stick_breaking_block(q_sb[:, i0:i1], k_sb[:, :i1], v_sb[:, :i1], i0))
        o_sb = jnp.concatenate(blocks, axis=1)
        s0 = jnp.zeros((xp.shape[0], RET_HEADS, HEAD_DIM, HEAD_DIM), state_ret.dtype)
        o_r, s_p = retention_prompt(q_r, k_r, v_r, s0, log_gamma)
        out = merge(o_sb, o_r, gate_sb, gate_r, sb_head_norm[l], ret_head_norm[l], w_out[l])
        xp = xp + rms_norm(out, norm_post[l])
        kp_l.append(k_sb)
        vp_l.append(v_sb)
        sp_l.append(s_p)
        h = rms_norm(xs, norm_pre[l])
        q_sb, k_sb, v_sb, gate_sb, q_r, k_r, v_r, gate_r = project(h, w_in[l], pos_s)
        k_all = jnp.concatenate([cache_sb_k[l].astype(k_sb.dtype), k_sb], axis=1)
        v_all = jnp.concatenate([cache_sb_v[l].astype(v_sb.dtype), v_sb], axis=1)
        o_sb = stick_breaking_block(q_sb, k_all, v_all, past)
        o_r, s_s = retention_chunk(q_r, k_r, v_r, state_ret[l], log_gamma)
        out = merge(o_sb, o_r, gate_sb, gate_r, sb_head_norm[l], ret_head_norm[l], w_out[l])
        xs = xs + rms_norm(out, norm_post[l])
        ks_l.append(k_sb)
        vs_l.append(v_sb)
        ss_l.append(s_s)
    return (xp, xs, jnp.stack(kp_l), jnp.stack(vp_l), jnp.stack(sp_l),
            jnp.stack(ks_l), jnp.stack(vs_l), jnp.stack(ss_l))
```

```python
import math
import numpy as np
import concourse.bass as bass
import concourse.mybir as mybir
from concourse.bass_utils import run_bass_kernel_spmd

F32 = mybir.dt.float32
BF16 = mybir.dt.bfloat16
AF = mybir.ActivationFunctionType
ALU = mybir.AluOpType

D = 2048
NDC = 16
T = 2048
NTT = 16
NS = 64
TS = T + NS
HD = 128
NH = 8
EPS = 1e-6
QSCALE = HD ** -0.5
SAME_SYNC = True


class Sem:
    def __init__(self, h):
        self.h = h
        self.val = 0


class Tok:
    __slots__ = ("sem", "val")

    def __init__(self, sem, val):
        self.sem = sem
        self.val = val


class Trk:
    def __init__(self, excl=False):
        self.w = None
        self.r = {}
        self.excl = excl


class TT:
    def __init__(self, ap, trk=None):
        self.ap = ap
        self.trk = trk if trk is not None else Trk()

    def __getitem__(self, idx):
        return TT(self.ap[idx], self.trk)

    def bitcast(self, dt):
        return TT(self.ap.bitcast(dt), self.trk)

    def rearrange(self, pat, **kw):
        return TT(self.ap.rearrange(pat, **kw), self.trk)


class Eng:
    def __init__(self, nc, name, h):
        self.name = name
        self.h = h
        self.sem = Sem(nc.alloc_semaphore("e_" + name))
        self.seen = {}

    def wait(self, tok):
        if tok is None:
            return
        if tok.sem is self.sem and (not SAME_SYNC or self.name in ("pe", "sp")):
            return
        if self.seen.get(tok.sem, 0) >= tok.val:
            return
        self.h.wait_ge(tok.sem.h, tok.val)
        self.seen[tok.sem] = tok.val

    def deps(self, reads, writes):
        for t in reads:
            self.wait(t.trk.w)
            if t.trk.excl:
                for sem, val in list(t.trk.r.items()):
                    self.wait(Tok(sem, val))
        for t in writes:
            self.wait(t.trk.w)
            for sem, val in list(t.trk.r.items()):
                self.wait(Tok(sem, val))

    def done(self, tok, reads, writes):
        for t in reads:
            r = t.trk.r
            if r.get(tok.sem, 0) < tok.val:
                r[tok.sem] = tok.val
        for t in writes:
            t.trk.w = tok
            t.trk.r = {}

    def op(self, fn, reads, writes, signal=True, predeps=True):
        if predeps:
            self.deps(reads, writes)
        inst = fn(self.h)
        if signal:
            self.sem.val += 1
            inst.then_inc(self.sem.h, 1)
            tok = Tok(self.sem, self.sem.val)
        else:
            tok = Tok(self.sem, self.sem.val + 1)
        self.done(tok, reads, writes)
        return tok

    def dma(self, out, in_, dsem, reads, writes, **kw):
        self.deps(reads, writes)
        self.h.dma_start(out=out, in_=in_, **kw).then_inc(dsem.h, 16)
        dsem.val += 16
        tok = Tok(dsem, dsem.val)
        self.done(tok, reads, writes)
        return tok


class Arena:
    def __init__(self, nc, nbytes):
        self.t = nc.alloc_sbuf_tensor("arena", [128, nbytes // 4], F32)
        self.off = 0
        self.total = nbytes

    def alloc(self, free_shape, dtype, trk=None):
        esz = 2 if dtype == BF16 else 4
        n = 1
        for s in free_shape:
            n *= s
        nb = (n * esz + 31) // 32 * 32
        assert self.off + nb <= self.total, f"arena overflow {self.off}+{nb}>{self.total}"
        ap = self.t[:, self.off // 4:(self.off + nb) // 4]
        self.off += nb
        if dtype == BF16:
            ap = ap.bitcast(BF16)
        ap = ap[:, 0:n]
        if len(free_shape) == 2:
            ap = ap.rearrange("p (a b) -> p a b", a=free_shape[0])
        elif len(free_shape) == 3:
            ap = ap.rearrange("p (a b c) -> p a b c", a=free_shape[0], b=free_shape[1])
        return TT(ap, trk)

    def mark(self):
        return self.off

    def release(self, m):
        self.off = m


CB_IDENT, CB_NTRI, CB_NONES, CB_MASK, CB_ONESD, CB_SMASK, CB_SNTRI = 0, 128, 256, 384, 512, 640, 704
CB_W = 768
CF_DM, CF_DSM, CF_DW, CF_DR, CF_DWS, CF_DRS, CF_MSK = 0, 1024, 1536, 1544, 1552, 1560, 1568
CF_W = 1572


def _consts():
    p = np.arange(128)[:, None].astype(np.float64)
    f = np.arange(128)[None, :].astype(np.float64)
    cb = np.zeros((128, CB_W), np.float32)
    cb[:, CB_IDENT:CB_IDENT + 128] = (p == f)
    cb[:, CB_NTRI:CB_NTRI + 128] = -1.0 * (p >= f)
    cb[:, CB_NONES:CB_NONES + 128] = -1.0
    cb[:, CB_MASK:CB_MASK + 128] = (p < f)
    cb[:, CB_ONESD:CB_ONESD + 128] = 1.0 / 128.0
    p64 = np.arange(64)[:, None]
    f64 = np.arange(64)[None, :]
    same = (p64 // 16) == (f64 // 16)
    cb[:64, CB_SMASK:CB_SMASK + 64] = same & ((p64 % 16) < (f64 % 16))
    cb[:64, CB_SNTRI:CB_SNTRI + 64] = -1.0 * (same & (p64 >= f64))
    cf = np.zeros((128, CF_W), np.float32)
    lg = np.log(1.0 - 2.0 ** (-5.0 - np.arange(NH, dtype=np.float64)))
    for h in range(NH):
        j = np.arange(128)[:, None]
        i = np.arange(128)[None, :]
        samechunk = (j // 64) == (i // 64)
        earlier = (j // 64) < (i // 64)
        m = np.where(samechunk, np.exp(lg[h] * np.abs(i - j)), np.where(earlier, np.exp(lg[h] * (i - j)), 0.0))
        cf[:, CF_DM + h * 128:CF_DM + (h + 1) * 128] = m
        ms = np.where(same, np.exp(lg[h] * np.abs((f64 % 16) - (p64 % 16))), 0.0)
        cf[:64, CF_DSM + h * 64:CF_DSM + (h + 1) * 64] = ms
        cf[:, CF_DW + h] = np.exp(lg[h] * (127.0 - np.arange(128)))
        cf[:, CF_DR + h] = np.exp(lg[h] * (np.arange(128) + 1.0))
        cf[:64, CF_DWS + h] = np.exp(lg[h] * (15.0 - (np.arange(64) % 16)))
        cf[:64, CF_DRS + h] = np.exp(lg[h] * ((np.arange(64) % 16) + 1.0))
    for s in range(4):
        cf[:64, CF_MSK + s] = (np.arange(64) // 16 == s)
    half = 64
    inv = (10000.0 ** (-np.arange(half, dtype=np.float32) / half)).astype(np.float32)
    rope = np.zeros((NTT + 1, 128, 256), np.float32)
    for tt in range(NTT + 1):
        if tt < NTT:
            pos = (tt * 128 + np.arange(128)).astype(np.float32)
        else:
            pos = (T + (np.arange(128) % 16)).astype(np.float32)
        ang = (pos[:, None] * inv[None, :]).astype(np.float32)
        c = np.cos(ang.astype(np.float64))
        s = np.sin(ang.astype(np.float64))
        rope[tt, :, 0:64] = c
        rope[tt, :, 64:128] = c * QSCALE
        rope[tt, :, 128:192] = s
        rope[tt, :, 192:256] = s * QSCALE
    gam = np.exp(lg)
    return cb, cf, rope, gam


class Builder:
    def __init__(self):
        nc = bass.Bass("TRN2", target_bir_lowering=False)
        self.nc = nc
        self.gam = _consts()[3]
        di = lambda n, s: nc.dram_tensor(n, s, F32, kind="ExternalInput").ap()
        do = lambda n, s: nc.dram_tensor(n, s, F32, kind="ExternalOutput").ap()
        self.xp = di("xp", [2, T, D])
        self.xs = di("xs", [NS, D])
        self.ck = di("ck", [4, T, NH * HD])
        self.cv = di("cv", [4, T, NH * HD])
        self.sr = di("sr", [4, NH, HD, HD])
        self.w_in = di("w_in", [D, 8192])
        self.w_out = di("w_out", [D, D])
        self.g1T_d = di("g1T", [128, NDC])
        self.gsb_d = di("gsb", [128, NH])
        self.gr_d = di("gr", [128, NH])
        self.gpost_d = di("gpost", [128, D])
        self.cb_d = di("cb", [128, CB_W])
        self.cf_d = di("cf", [128, CF_W])
        self.rope_d = di("rope", [NTT + 1, 128, 256])
        self.yp = do("yp", [2, T, D])
        self.ys = do("ys", [NS, D])
        self.kp = do("kp", [2, T, NH * HD])
        self.vp = do("vp", [2, T, NH * HD])
        self.spo = do("spo", [2, NH, HD, HD])
        self.ks = do("ks", [NS, NH * HD])
        self.vs = do("vs", [NS, NH * HD])
        self.ss = do("ss", [4, NH, HD, HD])
        self.wob = nc.dram_tensor("wob", [D, D], BF16, kind="Internal").ap()
        self.wob_t = TT(self.wob)
        self._wo_loaded = False
        import os as _os
        self.dbg = do("dbg", [128, 4096]) if _os.environ.get("KDEV_DBG") else None
        self._dsem_dbg = None

        self.pe = Eng(nc, "pe", nc.tensor)
        self.act = Eng(nc, "act", nc.scalar)
        self.dve = Eng(nc, "dve", nc.vector)
        self.pool = Eng(nc, "pool", nc.gpsimd)
        self.sp = Eng(nc, "sp", nc.sync)
        self.engs = [self.pe, self.act, self.dve, self.pool, self.sp]
        self.dsems = []
        self.out_toks = []

        total = (nc.sbuf_bytes_remaining // 32) * 32 - 512
        self.ar = Arena(nc, total)
        self.pb = [TT(nc.alloc_psum_tensor(f"pb{i}", [128, 512], F32)[:], Trk(excl=True)) for i in range(8)]
        self._rr = {}

    def dsem(self, name):
        s = Sem(self.nc.alloc_semaphore("d_" + name))
        self.dsems.append(s)
        return s

    def barrier(self):
        for e in self.engs:
            for x in self.engs:
                if x is not e and x.sem.val > 0:
                    e.wait(Tok(x.sem, x.sem.val))
            for d in self.dsems:
                if d.val > 0:
                    e.wait(Tok(d, d.val))

    def bank(self, group, ids):
        i = self._rr.get(group, 0)
        self._rr[group] = i + 1
        return self.pb[ids[i % len(ids)]]

    def mm(self, out, lhsT, rhs, start=True, stop=True, signal=True, predeps=True, reads=None, writes=None, skip=False):
        rd = reads if reads is not None else [lhsT, rhs]
        wr = writes if writes is not None else [out]
        kw = {}
        if skip:
            kw["skip_group_check"] = True
        return self.pe.op(lambda h: h.matmul(out.ap, lhsT=lhsT.ap, rhs=rhs.ap, start=start, stop=stop, **kw),
                          rd, wr, signal=signal, predeps=predeps)

    def mm_group(self, items, reads, writes, skip=False):
        self.pe.deps(reads, writes)
        tok = None
        n = len(items)
        for k, (o, l, r, st, sp_) in enumerate(items):
            tok = self.mm(o, l, r, start=st, stop=sp_, signal=(k == n - 1), predeps=False,
                          reads=reads, writes=writes, skip=skip)
        return tok

    def mm_group_gen(self, items, reads, writes, chunk=4, skip=False):
        self.pe.deps(reads, writes)
        n = len(items)
        for k, (o, l, r, st, sp_) in enumerate(items):
            self.mm(o, l, r, start=st, stop=sp_, signal=(k == n - 1), predeps=False,
                    reads=reads, writes=writes, skip=skip)
            if (k + 1) % chunk == 0 and k + 1 < n:
                yield

    def transpose(self, out, in_, ident, signal=True):
        return self.pe.op(lambda h: h.transpose(out.ap, in_.ap, ident.ap), [in_, ident], [out], signal=signal)

    def actf(self, out, in_, func, scale=1.0, bias=0.0, accum=None, extra_reads=()):
        kw = {}
        if accum is not None:
            kw["accum_out"] = accum.ap
        sc = scale.ap if isinstance(scale, TT) else scale
        rd = [in_] + list(extra_reads)
        if isinstance(scale, TT):
            rd.append(scale)
        wr = [out] + ([accum] if accum is not None else [])
        return self.act.op(lambda h: h.activation(out=out.ap, in_=in_.ap, func=func, bias=bias, scale=sc, **kw), rd, wr)

    def tt(self, eng, out, in0, in1, op):
        return eng.op(lambda h: h.tensor_tensor(out=out.ap, in0=in0.ap, in1=in1.ap, op=op), [in0, in1], [out])

    def ts(self, eng, out, in0, s1, op0, s2=None, op1=None):
        rd = [in0]
        a1 = s1
        a2 = s2
        if isinstance(s1, TT):
            rd.append(s1)
            a1 = s1.ap
        if isinstance(s2, TT):
            rd.append(s2)
            a2 = s2.ap
        if op1 is None:
            return eng.op(lambda h: h.tensor_scalar(out=out.ap, in0=in0.ap, scalar1=a1, scalar2=None, op0=op0), rd, [out])
        return eng.op(lambda h: h.tensor_scalar(out=out.ap, in0=in0.ap, scalar1=a1, scalar2=a2, op0=op0, op1=op1), rd, [out])

    def stt(self, out, in0, scalar, in1, op0, op1):
        rd = [in0, in1]
        a = scalar
        if isinstance(scalar, TT):
            rd.append(scalar)
            a = scalar.ap
        return self.dve.op(lambda h: h.scalar_tensor_tensor(out=out.ap, in0=in0.ap, scalar=a, in1=in1.ap, op0=op0, op1=op1),
                           rd, [out])

    def copy(self, eng, out, in_):
        if eng is self.act:
            return eng.op(lambda h: h.copy(out=out.ap, in_=in_.ap), [in_], [out])
        return eng.op(lambda h: h.tensor_copy(out=out.ap, in_=in_.ap), [in_], [out])

    def memset(self, eng, out, val):
        return eng.op(lambda h: h.memset(out.ap, val), [], [out])

    def store(self, dram_ap, src, dsem):
        tok = self.sp.dma(dram_ap, src.ap, dsem, [src], [])
        self.out_toks.append(tok)
        return tok

    def dump(self, src, col0):
        if self.dbg is None:
            return
        if self._dsem_dbg is None:
            self._dsem_dbg = self.dsem("dbg")
        p, n = src.ap.shape[0], src.ap.shape[1]
        tok = self.pool.dma(self.dbg[0:p, col0:col0 + n], src.ap, self._dsem_dbg, [src], [])
        self.out_toks.append(tok)

    def setup(self):
        ar = self.ar
        self.hT = ar.alloc([NDC, TS], BF16)
        self.yT = ar.alloc([NDC, TS], BF16)
        self.wring = [ar.alloc([NDC, 512], BF16) for _ in range(2)]
        self.wsem = [self.dsem(f"w{i}") for i in range(2)]
        self.cb = ar.alloc([CB_W], BF16)
        self.cf = ar.alloc([CF_W - 1536], F32)
        self.gains = ar.alloc([NDC + 2 * NH], F32)
        self.sq_q = ar.alloc([NH, NS], BF16)
        self.sq_k = ar.alloc([NH, NS], BF16)
        self.sq_g = ar.alloc([NH, NS], BF16)
        self.sq_v = ar.alloc([NH, HD], BF16)
        self.csem = self.dsem("const")
        m = ar.mark()
        cbf = ar.alloc([CB_W], F32)
        sp = self.sp
        sp.dma(cbf.ap, self.cb_d, self.csem, [], [cbf])
        sp.dma(self.cf.ap, self.cf_d[:, 1536:CF_W], self.csem, [], [self.cf])
        sp.dma(self.gains.ap[:, 0:NDC], self.g1T_d, self.csem, [], [self.gains])
        sp.dma(self.gains.ap[:, NDC:NDC + NH], self.gsb_d, self.csem, [], [self.gains])
        sp.dma(self.gains.ap[:, NDC + NH:NDC + 2 * NH], self.gr_d, self.csem, [], [self.gains])
        final = Tok(self.csem, self.csem.val)
        for t in (cbf, self.cf, self.gains):
            t.trk.w = final
        self.copy(self.dve, self.cb, cbf)
        self.barrier()
        ar.release(m)
        cb = self.cb
        self.ident = cb[:, CB_IDENT:CB_IDENT + 128]
        self.ntri = cb[:, CB_NTRI:CB_NTRI + 128]
        self.nones = cb[:, CB_NONES:CB_NONES + 128]
        self.mask = cb[:, CB_MASK:CB_MASK + 128]
        self.onesd = cb[:, CB_ONESD:CB_ONESD + 128]
        self.smask = cb[0:64, CB_SMASK:CB_SMASK + 64]
        self.sntri = cb[0:64, CB_SNTRI:CB_SNTRI + 64]
        self.work_mark = ar.mark()

    def precast_wout(self, part):
        if not hasattr(self, "_wobsem"):
            self._wobsem = self.dsem("wob")
        src = self.w_out.rearrange("(p r) (c n) -> p (r c) n", p=128, n=1024)
        dst = self.wob.rearrange("(p r) (c n) -> p (r c) n", p=128, n=1024)
        self.pool.h.dma_start(out=dst[:, part * 8:(part + 1) * 8, :], in_=src[:, part * 8:(part + 1) * 8, :]).then_inc(self._wobsem.h, 16)
        self._wobsem.val += 16
        self.wob_t.trk.w = Tok(self._wobsem, self._wobsem.val)

    def wout_view(self):
        return TT(self.hT.ap.rearrange("p a b -> p (a b)")[:, 0:NDC * D].rearrange("p (a b) -> p a b", a=NDC), self.hT.trk)

    def issue_wout_load(self):
        if not hasattr(self, "_wosem"):
            self._wosem = self.dsem("wo")
        Wo = self.wout_view()
        wv = self.wob.rearrange("(c p) n -> p c n", p=128)
        self.sp.deps([self.wob_t], [Wo])
        for c in range(2):
            self.sp.h.dma_start(out=Wo.ap[:, :, c * 1024:(c + 1) * 1024], in_=wv[:, :, c * 1024:(c + 1) * 1024]).then_inc(self._wosem.h, 16)
            self._wosem.val += 16
        self.sp.done(Tok(self._wosem, self._wosem.val), [self.wob_t], [Wo])
        self._wo_loaded = True

    def load_wgroup(self, gi, colblocks):
        slot = self.wring[gi % 2]
        sem = self.wsem[gi % 2]
        wv = self.w_in.rearrange("(c p) n -> p c n", p=128)
        self.pool.deps([], [slot])
        tok = None
        for k, c0 in enumerate(colblocks):
            self.pool.h.dma_start(out=slot.ap[:, :, k * 128:(k + 1) * 128], in_=wv[:, :, c0:c0 + 128]).then_inc(sem.h, 16)
            sem.val += 16
        tok = Tok(sem, sem.val)
        self.pool.done(tok, [], [slot])
        return slot

    def phase_a(self, seq, with_sample):
        ar = self.ar
        m = ar.mark()
        xs_ = [ar.alloc([D], F32) for _ in range(3)]
        xsem = [self.dsem(f"xa{i}") for i in range(3)] if not hasattr(self, "_xsem") else self._xsem
        self._xsem = xsem
        junk = ar.alloc([D], BF16)
        xna = [ar.alloc([D // 2], BF16) for _ in range(2)]
        xnb = [ar.alloc([D // 2], BF16) for _ in range(2)]
        stats = [ar.alloc([8], F32) for _ in range(2)]
        ntiles = NTT + (1 if with_sample else 0)
        g1 = self.gains

        def load(tt):
            x_t = xs_[tt % 3]
            if tt < NTT:
                self.sp.dma(x_t.ap, self.xp[seq, tt * 128:(tt + 1) * 128, :], xsem[tt % 3], [], [x_t])
            else:
                self.sp.dma(x_t.ap[0:NS, :], self.xs[:, :], xsem[tt % 3], [], [x_t])

        load(0)
        load(1)
        for tt in range(ntiles):
            if tt + 2 < ntiles:
                load(tt + 2)
            n = 128 if tt < NTT else NS
            t0 = tt * 128
            x_t = xs_[tt % 3][0:n]
            st = stats[tt % 2][0:n]
            self.actf(junk[0:n], x_t, AF.Square, accum=st[:, 0:1])
            self.actf(st[:, 1:2], st[:, 0:1], AF.Ln, scale=1.0 / D, bias=EPS)
            self.actf(st[:, 2:3], st[:, 1:2], AF.Exp, scale=-0.5)
            xh = [xna[tt % 2][0:n], xnb[tt % 2][0:n]]
            self.ts(self.dve, xh[0], x_t[:, 0:D // 2], st[:, 2:3], ALU.mult)
            self.actf(xh[1], x_t[:, D // 2:D], AF.Copy, scale=st[:, 2:3])
            for half in range(2):
                bk = self.bank("a", [0, 1, 2, 3])
                bb = bk.bitcast(BF16)
                for k in range(8):
                    self.transpose(bb[:, k * 128:k * 128 + n], xh[half][:, k * 128:(k + 1) * 128],
                                   self.ident[0:n, 0:n], signal=(k == 7))
                src = bb[:, 0:1024].rearrange("p (a b) -> p a b", a=8)[:, :, 0:n]
                gb = TT(g1.ap[:, half * 8:half * 8 + 8].unsqueeze(2).broadcast_to([128, 8, n]), g1.trk)
                self.tt(self.dve, self.hT[:, half * 8:half * 8 + 8, t0:t0 + n], src, gb, ALU.mult)
        self.barrier()
        ar.release(m)

    def silu_evac(self, bank_v, out, tmp):
        self.actf(tmp, bank_v, AF.Exp, scale=-1.0)
        self.actf(tmp, tmp, AF.Ln, bias=1.0)
        self.actf(tmp, tmp, AF.Exp, scale=-1.0)
        self.tt(self.dve, out, bank_v, tmp, ALU.mult)

    def epilogue(self, O, n, sg, gain, yout, sq, rr, tf, defer=None, ms_ids=(6, 7), defer_mm=False, copy_o=None):
        self.actf(sq[:, 0:n], O, AF.Square)
        if copy_o is None:
            copy_o = defer_mm
        if copy_o:
            self.copy(self.dve, tf[:, 0:n], O)
        box = {}

        def a2():
            box["ms"] = self.bank("ms" + str(ms_ids), list(ms_ids))
            self.mm(box["ms"][:, 0:n], self.onesd, sq[:, 0:n])

        if not defer_mm:
            a2()

        class _L:
            def __getitem__(s_, idx):
                return box["ms"][idx]
        ms = _L()

        def b1():
            self.actf(rr[:, 0:n], ms[:, 0:n], AF.Ln, bias=EPS)

        def b2():
            self.actf(rr[:, 0:n], rr[:, 0:n], AF.Exp, scale=-0.5)

        def b3():
            self.tt(self.dve, tf[:, 0:n], (tf[:, 0:n] if copy_o else O), rr[:, 0:n], ALU.mult)
            self.stt(yout, tf[:, 0:n], gain, sg, ALU.mult, ALU.mult)

        if defer is None:
            b1()
            b2()
            b3()
        else:
            if defer_mm:
                a2.is_a2 = True
                defer.append(a2)
            defer.extend([b1, b2, b3])

    def phase_c(self, seq, with_sample, gi0):
        ar = self.ar
        m = ar.mark()
        NPAR = 2
        qTb = [ar.alloc([512], BF16) for _ in range(4)]
        sgb = [ar.alloc([512], BF16) for _ in range(4)]
        kT123 = [ar.alloc([512], BF16) for _ in range(3)]
        V123 = [ar.alloc([4, HD], BF16) for _ in range(3)]
        kTb = [[ar.alloc([512], BF16)] + kT123 for _ in range(NPAR)]
        Vhb = [[ar.alloc([4, HD], BF16)] + V123 for _ in range(NPAR)]
        e1 = ar.alloc([512], F32)
        sp_ = [ar.alloc([512], BF16) for _ in range(2)]
        a_ = [ar.alloc([512], BF16) for _ in range(3)]
        R = ar.alloc([512], BF16)
        sq = ar.alloc([512], BF16)
        rr = ar.alloc([512], F32)
        stages = [ar.alloc([2, 256], F32) for _ in range(2)]
        kbf = ar.alloc([4, HD], BF16)
        self._stg = 0
        tf = ar.alloc([512], F32)
        tg = rr
        if not hasattr(self, "_stsem"):
            self._stsem = [self.dsem("stage0"), self.dsem("stage1")]
        hT = self.hT
        gsb = self.gains[:, NDC:NDC + NH]
        PB = [6, 7]

        def gen_inproj(h, W, tb):
            par = h % NPAR
            samp = tb == 4
            t0, n = (T, NS) if samp else (tb * 512, 512)

            def fm_group(ci, kind):
                bk = self.bank("cproj", PB)
                items = [(bk[:, 0:n], W[:, dc, ci * 128:(ci + 1) * 128], hT[:, dc, t0:t0 + n], dc == 0, dc == NDC - 1)
                         for dc in range(NDC)]
                yield from self.mm_group_gen(items, [W, hT], [bk])
                if kind == "q":
                    dst = self.sq_q[:, h, :] if samp else qTb[tb]
                    self.ts(self.dve, dst, bk[:, 0:n], QSCALE, ALU.mult)
                elif kind == "k":
                    dst = self.sq_k[:, h, :] if samp else kTb[par][tb]
                    self.copy(self.dve, dst, bk[:, 0:n])
                else:
                    dst = self.sq_g[:, h, :] if samp else sgb[tb]
                    tmp = tg[:, 0:n]
                    self.actf(tmp, bk[:, 0:n], AF.Exp, scale=-1.0)
                    yield
                    self.actf(tmp, tmp, AF.Ln, bias=1.0)
                    yield
                    self.actf(tmp, tmp, AF.Exp, scale=-1.0)
                    self.tt(self.dve, dst, bk[:, 0:n], tmp, ALU.mult)
                yield

            if not samp:
                for u2 in range(2):
                    bk = self.bank("cproj", PB)
                    items = []
                    for u in range(2):
                        tt_ = tb * 4 + u2 * 2 + u
                        for dc in range(NDC):
                            items.append((bk[:, u * 256:(u + 1) * 256], hT[:, dc, tt_ * 128:(tt_ + 1) * 128],
                                          W[:, dc, 128:384], dc == 0, dc == NDC - 1))
                    yield from self.mm_group_gen(items, [W, hT], [bk])
                    bv = bk.rearrange("p (a b) -> p a b", a=2)
                    stage = stages[self._stg % 2]
                    stsem = self._stsem[self._stg % 2]
                    self._stg += 1
                    self.copy(self.dve, stage, bv)
                    self.copy(self.dve, Vhb[par][tb][:, u2 * 2:u2 * 2 + 2, :], bv[:, :, 128:256])
                    self.copy(self.dve, kbf[:, u2 * 2:u2 * 2 + 2, :], bv[:, :, 0:128])
                    r0 = (tb * 4 + u2 * 2) * 128
                    kdst = self.kp[seq, r0:r0 + 256, h * 128:(h + 1) * 128].rearrange("(a p) c -> p a c", p=128)
                    vdst = self.vp[seq, r0:r0 + 256, h * 128:(h + 1) * 128].rearrange("(a p) c -> p a c", p=128)
                    self.store(kdst, stage[:, :, 0:128], stsem)
                    self.store(vdst, stage[:, :, 128:256], stsem)
                    yield
                yield from fm_group(0, "q")
                TBk = self.bank("cproj", PB).bitcast(BF16)
                for u in range(4):
                    self.transpose(TBk[:, u * 128:(u + 1) * 128], kbf[:, u, :], self.ident, signal=(u == 3))
                self.copy(self.dve, kTb[par][tb], TBk[:, 0:512])
                yield
                yield from fm_group(3, "g")
            else:
                yield from fm_group(0, "q")
                yield from fm_group(1, "k")
                yield from fm_group(3, "g")
                bk = self.bank("cproj", PB)
                items = [(bk[0:NS, 0:256], hT[:, dc, T:T + NS], W[:, dc, 128:384], dc == 0, dc == NDC - 1)
                         for dc in range(NDC)]
                yield from self.mm_group_gen(items, [W, hT], [bk])
                stage = stages[self._stg % 2]
                stsem = self._stsem[self._stg % 2]
                self._stg += 1
                self.copy(self.act, stage[0:NS, 0, :], bk[0:NS, 0:256])
                self.copy(self.dve, self.sq_v[0:NS, h, :], bk[0:NS, 128:256])
                self.store(self.ks[:, h * 128:(h + 1) * 128], stage[0:NS, 0, 0:128], stsem)
                self.store(self.vs[:, h * 128:(h + 1) * 128], stage[0:NS, 0, 128:256], stsem)
                yield

        def gen_attn(h, I):
            par = h % NPAR
            O = self.bank("cO", [3, 4])
            self.memset(self.dve, O, 0.0)
            self.memset(self.dve, R, 0.0)
            js = list(range(4 * I + 3, -1, -1))
            pend2 = None
            pend3 = None
            pend3n = None

            def geom(j):
                jj = j - 4 * I
                return (128 * jj if jj > 0 else 0), jj >= 0, 128 * jj

            def issue_z(j):
                c0, _, _ = geom(j)
                Z = self.bank("cZ", [0, 1, 2])
                kT_j = kTb[par][j // 4][:, (j % 4) * 128:(j % 4 + 1) * 128]
                self.mm(Z[:, c0:512], kT_j, qTb[I][:, c0:512], skip=True)
                return Z

            Znext = issue_z(js[0])
            for bi, j in enumerate(js):
                c0, diag, dcol = geom(j)
                Z = Znext
                if bi + 1 < len(js):
                    Znext = issue_z(js[bi + 1])
                e = e1
                spb = sp_[bi % 2]
                ab = a_[bi % 3]
                V_j = Vhb[par][j // 4][:, j % 4, :]
                self.actf(e[:, c0:512], Z[:, c0:512], AF.Exp)
                self.actf(spb[:, c0:512], e[:, c0:512], AF.Ln, bias=1.0)
                if diag:
                    self.tt(self.dve, spb[:, dcol:dcol + 128], spb[:, dcol:dcol + 128], self.mask, ALU.mult)

                def stage2(Z=Z, spb=spb, ab=ab, c0=c0, diag=diag, dcol=dcol, first=(bi == 0), last=(j == 0)):
                    items = [(Z[:, c0:512], self.ntri, spb[:, c0:512], False, first)]
                    rd = [self.cb, spb]
                    if not first:
                        items.append((Z[:, c0:512], self.nones, R[:, c0:512], False, True))
                        rd.append(R)
                    self.mm_group(items, rd, [Z], skip=True)
                    if not last:
                        self.tt(self.dve, R[:, c0:512], R[:, c0:512], spb[:, c0:512], ALU.add)
                    self.actf(ab[:, c0:512], Z[:, c0:512], AF.Exp)
                    if diag:
                        self.tt(self.dve, ab[:, dcol:dcol + 128], ab[:, dcol:dcol + 128], self.mask, ALU.mult)

                def stage3(ab=ab, c0=c0, V_j=V_j, last=(j == 0)):
                    self.mm(O[:, c0:512], V_j, ab[:, c0:512], start=False, stop=last, skip=True)

                if pend2 is not None:
                    pend2()
                if pend3 is not None:
                    pend3()
                if bi >= 1 and cdef:
                    cdef.pop(0)()
                pend3 = pend3n
                pend3n = stage3
                pend2 = stage2
                yield
            while cdef:
                cdef.pop(0)()
            pend2()
            if pend3 is not None:
                pend3()
            pend3n()
            self.epilogue(O, 512, sgb[I], gsb[:, h:h + 1], self.yT[:, h, I * 512:(I + 1) * 512], sq, rr, tf, defer=cdef, ms_ids=(5,), defer_mm=True, copy_o=False)
            yield

        cdef = []

        def interleave(main, nmain, fillers, nfill):
            fl = [f for f in fillers if f is not None]
            done_f = 0
            done_m = 0

            def fill_one():
                while fl:
                    try:
                        next(fl[0])
                        return True
                    except StopIteration:
                        fl.pop(0)
                return False

            for _ in main:
                done_m += 1
                while nfill and done_f * nmain < done_m * nfill:
                    if not fill_one():
                        break
                    done_f += 1
            while fill_one():
                pass

        Wh = {0: (self._wpre.pop(gi0) if gi0 in getattr(self, '_wpre', {}) else self.load_wgroup(gi0, self._sb_cols(0)))}
        for _ in gen_inproj(0, Wh[0], 0):
            pass
        for h in range(NH):
            Wn_cols = self._sb_cols(h + 1) if h + 1 < NH else self._ret_cols(0)
            Wh[h + 1] = self.load_wgroup(gi0 + h + 1, Wn_cols)
            if seq == 0 and 1 <= h <= 4:
                self.precast_wout(h - 1)
            W = Wh[h]
            for I in range(4):
                nblk = 4 * I + 5
                if I < 3:
                    fillers = [gen_inproj(h, W, I + 1)]
                    nfill = 27
                else:
                    fillers = []
                    nfill = 0
                    if with_sample:
                        fillers.append(gen_inproj(h, W, 4))
                        nfill += 18
                    if h + 1 < NH:
                        fillers.append(gen_inproj(h + 1, Wh[h + 1], 0))
                        nfill += 27
                interleave(gen_attn(h, I), nblk, fillers, nfill)
        while cdef:
            cdef.pop(0)()
        W = Wh[NH]
        self.barrier()
        ar.release(m)
        return W

    def _sb_cols(self, h):
        return [0 * 1024 + h * 128, 1 * 1024 + h * 128, 2 * 1024 + h * 128, 3 * 1024 + h * 128]

    def _ret_cols(self, h):
        return [4 * 1024 + h * 128, 5 * 1024 + h * 128, 6 * 1024 + h * 128, 7 * 1024 + h * 128]

    def phase_d(self, seq, with_sample, gi0, W0, next_cols):
        ar = self.ar
        m = ar.mark()
        hT = self.hT
        cfd = ar.alloc([1536], F32)
        if not hasattr(self, "_rtsem"):
            self._rtsem = [self.dsem(f"rt{i}") for i in range(3)]
            self._sosem = [self.dsem(f"so{i}") for i in range(2)]
            self._sinsem = self.dsem("sin")
            self._snsem = self.dsem("snew")
            self._cfdsem = self.dsem("cfd")
        self.sp.dma(cfd.ap, self.cf_d[:, CF_DM:CF_DM + 1536], self._cfdsem, [], [cfd])
        sgb = [ar.alloc([512], BF16) for _ in range(4)] + [ar.alloc([NS], BF16)]
        tf = ar.alloc([512], F32)
        sq = ar.alloc([512], BF16)
        rr = ar.alloc([512], F32)
        tg = rr
        ra = [ar.alloc([4, 2, 64], F32) for _ in range(2)]
        qk = [ar.alloc([2, HD], BF16) for _ in range(4)]
        vt = [ar.alloc([HD], BF16) for _ in range(5)]
        qd = [ar.alloc([HD], BF16) for _ in range(4)]
        kd = [ar.alloc([HD], BF16) for _ in range(5)]
        qkT = [ar.alloc([3 * HD], BF16) for _ in range(3)]
        scm = [ar.alloc([HD], BF16) for _ in range(3)]
        S32 = [ar.alloc([HD], F32) for _ in range(2)]
        Sbf = [ar.alloc([HD], BF16) for _ in range(2)]
        rt = [ar.alloc([2, 2, 64], F32) for _ in range(2)]
        rtsem = self._rtsem
        if with_sample:
            Sin32 = ar.alloc([4, HD], F32)
            Sinb = ar.alloc([4, HD], BF16)
            Snew = Sin32
            kdm = ar.alloc([4, HD], BF16)
            tfs = ar.alloc([NS], F32)
        gr = self.gains[:, NDC + NH:NDC + 2 * NH]
        cf = self.cf
        ntiles = NTT + (1 if with_sample else 0)
        tblocks = [(i * 512, 512) for i in range(4)] + ([(T, NS)] if with_sample else [])
        steps = [(h, t) for h in range(NH) for t in range(ntiles)]
        Wh = {0: W0}
        st = {}

        def gate_block(h, b):
            t0, n = tblocks[b]
            W = Wh[h]
            bk = self.bank("dproj", [5])
            items = [(bk[:, 0:n], W[:, dc, 384:512], hT[:, dc, t0:t0 + n], dc == 0, dc == NDC - 1) for dc in range(NDC)]
            self.mm_group(items, [W, hT], [bk])
            self.silu_evac(bk[:, 0:n], sgb[b][:, 0:n], tg[:, 0:n])

        def S1(k):
            h, tt_ = steps[k]
            W = Wh[h]
            if tt_ == 0:
                if h + 1 < NH:
                    Wh[h + 1] = self.load_wgroup(gi0 + h + 1, self._ret_cols(h + 1))
                elif next_cols is not None:
                    Wh[h + 1] = self.load_wgroup(gi0 + h + 1, next_cols)
            if with_sample and tt_ == 8:
                self.sp.dma(Sin32.ap, self.sr[:, h, :, :].rearrange("s d e -> d s e"), self._sinsem, [], [Sin32])
                self.copy(self.act, Sinb, Sin32)
            n = 128 if tt_ < NTT else NS
            t0 = tt_ * 128
            samp = tt_ >= NTT
            rtab = rt[k % 2]
            self.sp.dma(rtab.ap, self.rope_d[tt_].rearrange("p (a b c) -> p a b c", a=2, b=2), rtsem[k % 2], [], [rtab])
            P = self.bank("dP", [0, 1])
            items = [(P[0:n, 0:384], hT[:, dc, t0:t0 + n], W[:, dc, 0:384], dc == 0, dc == NDC - 1) for dc in range(NDC)]
            self.mm_group(items, [W, hT], [P])
            st[k] = dict(P=P, rtab=rtab, n=n, samp=samp, t0=t0)

        def S1b(k):
            h, tt_ = steps[k]
            d0 = st[k]
            P, rtab, n, samp, t0 = d0["P"], d0["rtab"], d0["n"], d0["samp"], d0["t0"]
            Pv = P[0:n, 0:256].rearrange("p (a b c) -> p a b c", a=2, b=2)
            X1 = Pv[:, :, 0, :]
            X2 = Pv[:, :, 1, :]
            Ct = rtab[0:n, 0]
            St = rtab[0:n, 1]
            r_ = ra[k % 2][0:n]
            tabv = rtab[0:n]
            r4 = TT(r_.ap.rearrange("p (u t) b c -> p u t b c", u=2), r_.trk)
            for u, X in ((0, X1), (1, X2)):
                xb_ = TT(X.ap.unsqueeze(1).broadcast_to([n, 2, 2, 64]), X.trk)
                self.tt(self.dve, r4[:, u], xb_, tabv, ALU.mult)
            qk_ = qk[k % 4][0:n]
            self.tt(self.dve, qk_[:, :, 0:64], r_[:, 0], r_[:, 3], ALU.subtract)
            self.tt(self.dve, qk_[:, :, 64:128], r_[:, 1], r_[:, 2], ALU.add)
            st[k]["qk"] = qk_

        def S1c(k):
            h, tt_ = steps[k]
            d0 = st[k]
            P, n, samp, qk_ = d0["P"], d0["n"], d0["samp"], d0["qk"]
            v_ = vt[k % 5][0:n]
            self.copy(self.act, v_, P[0:n, 256:384])
            qd_ = qd[k % 4][0:n]
            kd_ = kd[k % 5][0:n]
            if not samp:
                self.actf(qd_, qk_[:, 0, :], AF.Copy, scale=cf[0:n, (CF_DR - 1536) + h:(CF_DR - 1536) + h + 1])
                self.actf(kd_, qk_[:, 1, :], AF.Copy, scale=cf[0:n, (CF_DW - 1536) + h:(CF_DW - 1536) + h + 1])
            else:
                self.actf(qd_, qk_[:, 0, :], AF.Copy, scale=cf[0:n, (CF_DRS - 1536) + h:(CF_DRS - 1536) + h + 1])
                for s in range(4):
                    self.ts(self.dve, kdm[0:n, s, :], qk_[:, 1, :], cf[0:n, (CF_DWS - 1536) + h:(CF_DWS - 1536) + h + 1], ALU.mult,
                            cf[0:n, (CF_MSK - 1536) + s:(CF_MSK - 1536) + s + 1], ALU.mult)
            st[k].update(dict(v=v_, qk=qk_, qd=qd_, kd=kd_))
            if tt_ < len(tblocks):
                gate_block(h, tt_)

        def S2(k):
            h, tt_ = steps[k]
            d = st[k]
            n, samp = d["n"], d["samp"]
            TB = self.bank("dT", [2]).bitcast(BF16)
            idn = self.ident[0:n, 0:n]
            self.transpose(TB[:, 0:n], d["qk"][:, 0, :], idn, signal=False)
            self.transpose(TB[:, n:2 * n], d["qd"], idn, signal=False)
            self.transpose(TB[:, 2 * n:3 * n], d["qk"][:, 1, :], idn, signal=True)
            qkT_ = qkT[k % 3]
            self.copy(self.dve, qkT_[:, 0:3 * n], TB[:, 0:3 * n])
            d["qkT"] = qkT_

        def S2b(k):
            h, tt_ = steps[k]
            d = st[k]
            n, samp = d["n"], d["samp"]
            qkT_ = d["qkT"]
            SC = self.bank("dS", [3])
            self.mm(SC[0:n, 0:n], qkT_[:, 2 * n:3 * n], qkT_[:, 0:n])
            scm_ = scm[k % 3][0:n, 0:n]
            if not samp:
                self.tt(self.dve, scm_, SC[0:n, 0:n], cfd[:, h * 128:(h + 1) * 128], ALU.mult)
            else:
                self.tt(self.dve, scm_, SC[0:n, 0:n], cfd[0:n, 1024 + h * 64:1024 + (h + 1) * 64], ALU.mult)
            d["qkT"] = qkT_
            d["scm"] = scm_

        def S3(k):
            h, tt_ = steps[k]
            d = st.pop(k)
            n, samp = d["n"], d["samp"]
            v_, scm_, qkT_, kd_ = d["v"], d["scm"], d["qkT"], d["kd"]
            qdT_ = qkT_[:, n:2 * n]
            S = S32[h % 2]
            if not samp:
                if tt_ % 4 == 0:
                    self._dO = self.bank("dO", [4])
                O = self._dO
                Oc = O[:, (tt_ % 4) * 128:(tt_ % 4 + 1) * 128]
                items = [(Oc, v_, scm_, True, tt_ == 0)]
                rd = [v_, scm_]
                if tt_ > 0:
                    items.append((Oc, Sbf[(tt_ - 1) % 2], qdT_, False, True))
                    rd += [Sbf[(tt_ - 1) % 2], qkT_]
                self.mm_group(items, rd, [O])
                U = self.bank("dU", [7])
                self.mm(U[:, 0:128], kd_, v_)
                if tt_ == 0:
                    self.copy(self.dve, S, U[:, 0:128])
                else:
                    self.stt(S, S, float(self.gam[h] ** 128), U[:, 0:128], ALU.mult, ALU.add)
                if tt_ < NTT - 1:
                    self.copy(self.act, Sbf[tt_ % 2], S)
                else:
                    self.store(self.spo[seq, h, :, :], S, self._sosem[h % 2])
                if tt_ % 4 == 3:
                    I = tt_ // 4
                    self.epilogue(O, 512, sgb[I], gr[:, h:h + 1], self.yT[:, NH + h, I * 512:(I + 1) * 512], sq, rr, tf, defer=deferred, ms_ids=(6,), defer_mm=True)
            else:
                Os = self.bank("dO", [4])
                items = [(Os[:, 0:NS], v_, scm_, True, False)]
                for s in range(4):
                    items.append((Os[:, s * 16:(s + 1) * 16], Sinb[:, s, :], qdT_[:, s * 16:(s + 1) * 16], False, s == 3))
                self.mm_group(items, [v_, scm_, Sinb, qkT_], [Os], skip=True)
                U = self.bank("dU", [7])
                items = [(U[:, s * 128:(s + 1) * 128], kdm[0:NS, s, :], v_, s == 0, s == 3) for s in range(4)]
                self.mm_group(items, [kdm, v_], [U], skip=True)
                self.stt(Snew.rearrange("p a b -> p (a b)"), Sin32.rearrange("p a b -> p (a b)"), float(self.gam[h] ** 16),
                         U, ALU.mult, ALU.add)
                self.store(self.ss[:, h, :, :].rearrange("s d e -> d s e"), Snew, self._snsem)
                self.epilogue(Os[:, 0:NS], NS, sgb[4], gr[:, h:h + 1], self.yT[:, NH + h, T:T + NS], sq, rr, tfs,
                              defer=deferred, ms_ids=(6,), defer_mm=True)

        nst = len(steps)
        deferred = []
        for k in range(nst + 3):
            if 0 <= k - 2 < nst:
                S2(k - 2)
            if k < nst:
                S1(k)
                if k == nst - 1 and not with_sample:
                    self.issue_wout_load()
            if deferred and getattr(deferred[0], "is_a2", False):
                deferred.pop(0)()
            if 0 <= k - 1 < nst:
                S1c(k - 1)
            if 0 <= k - 2 < nst:
                S2b(k - 2)
            if 0 <= k - 3 < nst:
                S3(k - 3)
            if k < nst:
                S1b(k)
            while deferred and not getattr(deferred[0], "is_a2", False):
                deferred.pop(0)()
        while deferred:
            deferred.pop(0)()
        self.barrier()
        ar.release(m)
        return Wh.get(NH)

    def phase_s(self):
        ar = self.ar
        m = ar.mark()
        hreg = self.hT.ap.rearrange("p a b -> p (a b)")

        def carve(off, n):
            return hreg[:, off:off + n]

        KC = [TT(carve(i * 8192, 8192).rearrange("p (s j d) -> p s j d", s=4, j=NTT), Trk()) for i in range(2)]
        VC = [TT(carve(16384 + i * 8192, 8192).rearrange("p (s j d) -> p s j d", s=4, j=NTT), Trk()) for i in range(2)]
        KT = ar.alloc([4, T], BF16)
        e_b = [ar.alloc([512], F32) for _ in range(2)]
        sp_all = ar.alloc([NTT, NS], BF16)
        R_all = ar.alloc([NTT, NS], BF16)
        a_all = ar.alloc([NTT, NS], BF16)
        e_n = ar.alloc([NS], F32)
        sp_n = ar.alloc([NS], BF16)
        a_n = ar.alloc([NS], BF16)
        sq = ar.alloc([512], BF16)
        rr = ar.alloc([512], F32)
        tf = ar.alloc([512], F32)
        if not hasattr(self, "_kcsem"):
            self._kcsem = [self.dsem(f"kc{i}") for i in range(2)]
            self._vcsem = [self.dsem(f"vc{i}") for i in range(2)]
        gsb = self.gains[:, NDC:NDC + NH]

        def load(h):
            kc = KC[h % 2]
            vc = VC[h % 2]
            self.pool.deps([], [kc, vc])
            for s in range(4):
                self.pool.h.dma_start(out=kc.ap[:, s], in_=self.ck[s, :, h * 128:(h + 1) * 128].rearrange("(j p) d -> p j d", p=128)
                                      ).then_inc(self._kcsem[h % 2].h, 16)
                self._kcsem[h % 2].val += 16
            for s in range(4):
                self.pool.h.dma_start(out=vc.ap[:, s], in_=self.cv[s, :, h * 128:(h + 1) * 128].rearrange("(j p) d -> p j d", p=128)
                                      ).then_inc(self._vcsem[h % 2].h, 16)
                self._vcsem[h % 2].val += 16
            self.pool.done(Tok(self._kcsem[h % 2], self._kcsem[h % 2].val), [], [kc])
            self.pool.done(Tok(self._vcsem[h % 2], self._vcsem[h % 2].val), [], [vc])

        self.memset(self.dve, R_all, 0.0)
        load(0)
        for h in range(NH):
            if h + 1 < NH:
                load(h + 1)
            kc = KC[h % 2]
            vc = VC[h % 2]
            qs = self.sq_q[:, h, :]
            ks_ = self.sq_k[:, h, :]
            O = self.bank("sO", [3])
            self.memset(self.dve, O[:, 0:NS], 0.0)
            Zn = self.bank("sZn", [6])
            self.mm(Zn[0:NS, 0:NS], ks_, qs, skip=True)
            self.actf(e_n[0:NS], Zn[0:NS, 0:NS], AF.Exp)
            self.actf(sp_n[0:NS], e_n[0:NS], AF.Ln, bias=1.0)
            self.tt(self.dve, R_all[0:NS, NTT - 1, :], sp_n[0:NS], self.smask, ALU.mult)
            self.mm(Zn[0:NS, 0:NS], self.sntri, R_all[0:NS, NTT - 1, :], start=False, stop=True, skip=True)
            self.actf(a_n[0:NS], Zn[0:NS, 0:NS], AF.Exp)
            self.tt(self.dve, a_n[0:NS], a_n[0:NS], self.smask, ALU.mult)
            self.mm(O[:, 0:NS], self.sq_v[0:NS, h, :], a_n[0:NS], start=False, stop=False, skip=True)
            for s in range(4):
                for g in range(2):
                    bk = self.bank("sT", [0, 1, 2]).bitcast(BF16)
                    for k in range(8):
                        j = g * 8 + k
                        self.transpose(bk[:, k * 128:(k + 1) * 128], kc[:, s, j, :], self.ident, signal=(k == 7))
                    eng = self.dve if (s * 2 + g) % 2 == 0 else self.act
                    self.copy(eng, KT[:, s, g * 1024:(g + 1) * 1024], bk[:, 0:1024])
            Zb = {}
            for half in (1, 0):
                Z = self.bank("sZ", [4, 5])
                Zb[half] = Z
                items = []
                for jl in range(8):
                    j = half * 8 + jl
                    for s in range(4):
                        items.append((Z[:, jl * 64 + s * 16:jl * 64 + (s + 1) * 16], KT[:, s, j * 128:(j + 1) * 128],
                                      qs[:, s * 16:(s + 1) * 16], len(items) == 0, False))
                self.mm_group(items, [KT, self.sq_q], [Z], skip=True)
                e = e_b[half]
                spv = sp_all[:, half * 8:half * 8 + 8, :].rearrange("p a b -> p (a b)")
                self.actf(e, Z, AF.Exp)
                self.actf(spv, e, AF.Ln, bias=1.0)
                for j in range(half * 8 + 7, half * 8 - 1, -1):
                    if j == NTT - 1:
                        continue
                    self.tt(self.dve, R_all[:, j, :], R_all[:, j + 1, :], sp_all[:, j + 1, :], ALU.add)
                Rv = R_all[:, half * 8:half * 8 + 8, :].rearrange("p a b -> p (a b)")
                items = [(Z, self.ntri, spv, False, False), (Z, self.nones, Rv, False, True)]
                self.mm_group(items, [self.cb, sp_all, R_all], [Z], skip=True)
                av = a_all[:, half * 8:half * 8 + 8, :].rearrange("p a b -> p (a b)")
                self.actf(av, Z, AF.Exp)
                items = []
                for jl in range(8):
                    j = half * 8 + jl
                    for s in range(4):
                        items.append((O[:, s * 16:(s + 1) * 16], vc[:, s, j, :], a_all[:, j, s * 16:(s + 1) * 16], False,
                                      (half == 0 and jl == 7 and s == 3)))
                self.mm_group(items, [vc, a_all], [O], skip=True)
            self.epilogue(O[:, 0:NS], NS, self.sq_g[:, h, :], gsb[:, h:h + 1], self.yT[:, h, T:T + NS], sq, rr, tf)
        self.barrier()
        ar.release(m)

    def phase_e(self, seq, with_sample):
        ar = self.ar
        m = ar.mark()
        if not hasattr(self, "_gpsem"):
            self._gpsem = self.dsem("gp")
            self._xesem = [self.dsem(f"xe{i}") for i in range(2)]
            self._yosem = [self.dsem(f"yo{i}") for i in range(2)]
        if not self._wo_loaded:
            self.issue_wout_load()
        self._wo_loaded = False
        Wo = self.wout_view()
        gp = ar.alloc([D], F32)
        self.sp.dma(gp.ap, self.gpost_d, self._gpsem, [], [gp])
        xs_ = [ar.alloc([D], F32) for _ in range(2)]
        tfs = [ar.alloc([512], F32) for _ in range(2)]
        junk = ar.alloc([512], BF16)
        stat = ar.alloc([16], F32)
        ntiles = NTT + (1 if with_sample else 0)

        def load(tt_):
            x_t = xs_[tt_ % 2]
            if tt_ < NTT:
                self.sp.dma(x_t.ap, self.xp[seq, tt_ * 128:(tt_ + 1) * 128, :], self._xesem[tt_ % 2], [], [x_t])
            else:
                self.sp.dma(x_t.ap[0:NS, :], self.xs[:, :], self._xesem[tt_ % 2], [], [x_t])

        load(0)
        for tt_ in range(ntiles):
            if tt_ + 1 < ntiles:
                load(tt_ + 1)
            n = 128 if tt_ < NTT else NS
            t0 = tt_ * 128
            x_t = xs_[tt_ % 2]
            bks = []
            st = stat[0:n, (tt_ % 2) * 8:(tt_ % 2) * 8 + 8]
            for c in range(4):
                bk = self.bank("e", [0, 1, 2, 3, 4, 5, 6, 7])
                items = [(bk[0:n, :], self.yT[:, mc, t0:t0 + n], Wo[:, mc, c * 512:(c + 1) * 512], mc == 0, mc == NDC - 1)
                         for mc in range(NDC)]
                self.mm_group(items, [self.yT, Wo], [bk])
                self.actf(junk[0:n], bk[0:n, :], AF.Square, accum=st[:, c:c + 1])
                bks.append(bk)
            self.tt(self.dve, st[:, 4:5], st[:, 0:1], st[:, 1:2], ALU.add)
            self.tt(self.dve, st[:, 5:6], st[:, 2:3], st[:, 3:4], ALU.add)
            self.tt(self.dve, st[:, 4:5], st[:, 4:5], st[:, 5:6], ALU.add)
            self.actf(st[:, 6:7], st[:, 4:5], AF.Ln, scale=1.0 / D, bias=EPS)
            self.actf(st[:, 7:8], st[:, 6:7], AF.Exp, scale=-0.5)
            for c in range(4):
                tfc = tfs[c % 2]
                self.stt(tfc[0:n], bks[c][0:n, :], st[:, 7:8], gp[0:n, c * 512:(c + 1) * 512], ALU.mult, ALU.mult)
                self.tt(self.pool, x_t[0:n, c * 512:(c + 1) * 512], x_t[0:n, c * 512:(c + 1) * 512], tfc[0:n], ALU.add)
            if tt_ < NTT:
                self.store(self.yp[seq, t0:t0 + 128, :], x_t, self._yosem[tt_ % 2])
            else:
                self.store(self.ys[:, :], x_t[0:NS], self._yosem[tt_ % 2])
        self.barrier()
        ar.release(m)

    def build(self):
        with self.nc.Block():
            self.setup()
            gi = 0
            import os as _os
            stop = _os.environ.get("KDEV_STOP", "")
            seqs = [int(c) for c in _os.environ.get("KDEV_SEQS", "01")]
            self._wpre = {}
            for seq in seqs:
                ws = (seq == 1)
                if stop == "setup":
                    continue
                if gi not in self._wpre:
                    self._wpre[gi] = self.load_wgroup(gi, self._sb_cols(0))
                self.phase_a(seq, ws)
                if stop == "a":
                    continue
                W = self.phase_c(seq, ws, gi)
                gi += NH
                if stop == "c":
                    continue
                W = self.phase_d(seq, ws, gi, W, None)
                gi += NH
                if stop == "d":
                    continue
                if ws:
                    self.phase_s()
                if stop == "s":
                    continue
                if seq == seqs[0] and len(seqs) > 1:
                    self._wpre[gi] = self.load_wgroup(gi, self._sb_cols(0))
                self.phase_e(seq, ws)
            for tok in self.out_toks:
                self.sp.wait(tok)
            self.barrier()
        return self.nc


_CACHE = {}


def _get_prog():
    if "nc" not in _CACHE:
        b = Builder()
        _CACHE["nc"] = b.build()
    return _CACHE["nc"]


def make_in_maps(x_prompt, x_sample, cache_sb_k, cache_sb_v, state_ret, norm_pre, w_in, sb_head_norm,
                 ret_head_norm, w_out, norm_post, cores):
    cb, cf, rope, _ = _consts()
    f = lambda a: np.ascontiguousarray(a, dtype=np.float32)
    g1T = f(norm_pre[0].reshape(NDC, 128).T)
    gsb = f(sb_head_norm[0].reshape(NH, 128).T)
    gr = f(ret_head_norm[0].reshape(NH, 128).T)
    gpost = f(np.broadcast_to(norm_post[0][None, :], (128, D)))
    w_in0 = f(w_in[0])
    w_out0 = f(w_out[0])
    maps = []
    for c in cores:
        maps.append({
            "xp": f(x_prompt[2 * c:2 * c + 2]),
            "xs": f(x_sample[4 * c:4 * c + 4].reshape(NS, D)),
            "ck": f(cache_sb_k[0, 4 * c:4 * c + 4].reshape(4, T, NH * HD)),
            "cv": f(cache_sb_v[0, 4 * c:4 * c + 4].reshape(4, T, NH * HD)),
            "sr": f(state_ret[0, 4 * c:4 * c + 4]),
            "w_in": w_in0, "w_out": w_out0, "g1T": g1T, "gsb": gsb, "gr": gr, "gpost": gpost,
            "cb": cb, "cf": cf, "rope": rope,
        })
    return maps


def assemble(results, ncores):
    yp = np.concatenate([r["yp"] for r in results], axis=0)
    ys = np.concatenate([r["ys"].reshape(4, 16, D) for r in results], axis=0)
    kp = np.concatenate([r["kp"].reshape(2, T, NH, HD) for r in results], axis=0)[None]
    vp = np.concatenate([r["vp"].reshape(2, T, NH, HD) for r in results], axis=0)[None]
    spo = np.concatenate([r["spo"] for r in results], axis=0)[None]
    ks = np.concatenate([r["ks"].reshape(4, 16, NH, HD) for r in results], axis=0)[None]
    vs = np.concatenate([r["vs"].reshape(4, 16, NH, HD) for r in results], axis=0)[None]
    ss = np.concatenate([r["ss"] for r in results], axis=0)[None]
    return (yp, ys, kp, vp, spo, ks, vs, ss)


def kernel(x_prompt, x_sample, cache_sb_k, cache_sb_v, state_ret, norm_pre, w_in, sb_head_norm,
           ret_head_norm, w_out, norm_post):
    args = [np.asarray(a) for a in (x_prompt, x_sample, cache_sb_k, cache_sb_v, state_ret, norm_pre, w_in,
                                    sb_head_norm, ret_head_norm, w_out, norm_post)]
    nc = _get_prog()
    cores = list(range(8))
    maps = make_in_maps(*args, cores)
    res = run_bass_kernel_spmd(nc, maps, core_ids=cores)
    outs = assemble(res.results, 8)
    return tuple(np.ascontiguousarray(o, dtype=np.float32) for o in outs)
```

```python
import math
import numpy as np
import concourse.bass as bass
import concourse.mybir as mybir
from concourse.bass_utils import run_bass_kernel_spmd

F32 = mybir.dt.float32
BF16 = mybir.dt.bfloat16
AF = mybir.ActivationFunctionType
ALU = mybir.AluOpType

D = 2048
NDC = 16
T = 2048
NTT = 16
NS = 64
TS = T + NS
HD = 128
NH = 8
EPS = 1e-6
QSCALE = HD ** -0.5
SAME_SYNC = True


class Sem:
    def __init__(self, h):
        self.h = h
        self.val = 0


class Tok:
    __slots__ = ("sem", "val")

    def __init__(self, sem, val):
        self.sem = sem
        self.val = val


class Trk:
    def __init__(self, excl=False):
        self.w = None
        self.r = {}
        self.excl = excl


class TT:
    def __init__(self, ap, trk=None):
        self.ap = ap
        self.trk = trk if trk is not None else Trk()

    def __getitem__(self, idx):
        return TT(self.ap[idx], self.trk)

    def bitcast(self, dt):
        return TT(self.ap.bitcast(dt), self.trk)

    def rearrange(self, pat, **kw):
        return TT(self.ap.rearrange(pat, **kw), self.trk)


class Eng:
    def __init__(self, nc, name, h):
        self.name = name
        self.h = h
        self.sem = Sem(nc.alloc_semaphore("e_" + name))
        self.seen = {}

    def wait(self, tok):
        if tok is None:
            return
        if tok.sem is self.sem and (not SAME_SYNC or self.name in ("pe", "sp")):
            return
        if self.seen.get(tok.sem, 0) >= tok.val:
            return
        self.h.wait_ge(tok.sem.h, tok.val)
        self.seen[tok.sem] = tok.val

    def deps(self, reads, writes):
        for t in reads:
            self.wait(t.trk.w)
            if t.trk.excl:
                for sem, val in list(t.trk.r.items()):
                    self.wait(Tok(sem, val))
        for t in writes:
            self.wait(t.trk.w)
            for sem, val in list(t.trk.r.items()):
                self.wait(Tok(sem, val))

    def done(self, tok, reads, writes):
        for t in reads:
            r = t.trk.r
            if r.get(tok.sem, 0) < tok.val:
                r[tok.sem] = tok.val
        for t in writes:
            t.trk.w = tok
            t.trk.r = {}

    def op(self, fn, reads, writes, signal=True, predeps=True):
        if predeps:
            self.deps(reads, writes)
        inst = fn(self.h)
        if signal:
            self.sem.val += 1
            inst.then_inc(self.sem.h, 1)
            tok = Tok(self.sem, self.sem.val)
        else:
            tok = Tok(self.sem, self.sem.val + 1)
        self.done(tok, reads, writes)
        return tok

    def dma(self, out, in_, dsem, reads, writes, **kw):
        self.deps(reads, writes)
        self.h.dma_start(out=out, in_=in_, **kw).then_inc(dsem.h, 16)
        dsem.val += 16
        tok = Tok(dsem, dsem.val)
        self.done(tok, reads, writes)
        return tok


class Arena:
    def __init__(self, nc, nbytes):
        self.t = nc.alloc_sbuf_tensor("arena", [128, nbytes // 4], F32)
        self.off = 0
        self.total = nbytes

    def alloc(self, free_shape, dtype, trk=None):
        esz = 2 if dtype == BF16 else 4
        n = 1
        for s in free_shape:
            n *= s
        nb = (n * esz + 31) // 32 * 32
        assert self.off + nb <= self.total, f"arena overflow {self.off}+{nb}>{self.total}"
        ap = self.t[:, self.off // 4:(self.off + nb) // 4]
        self.off += nb
        if dtype == BF16:
            ap = ap.bitcast(BF16)
        ap = ap[:, 0:n]
        if len(free_shape) == 2:
            ap = ap.rearrange("p (a b) -> p a b", a=free_shape[0])
        elif len(free_shape) == 3:
            ap = ap.rearrange("p (a b c) -> p a b c", a=free_shape[0], b=free_shape[1])
        return TT(ap, trk)

    def mark(self):
        return self.off

    def release(self, m):
        self.off = m


CB_IDENT, CB_NTRI, CB_NONES, CB_MASK, CB_ONESD, CB_SMASK, CB_SNTRI = 0, 128, 256, 384, 512, 640, 704
CB_W = 768
CF_DM, CF_DSM, CF_DW, CF_DR, CF_DWS, CF_DRS, CF_MSK = 0, 1024, 1536, 1544, 1552, 1560, 1568
CF_W = 1572


def _consts():
    p = np.arange(128)[:, None].astype(np.float64)
    f = np.arange(128)[None, :].astype(np.float64)
    cb = np.zeros((128, CB_W), np.float32)
    cb[:, CB_IDENT:CB_IDENT + 128] = (p == f)
    cb[:, CB_NTRI:CB_NTRI + 128] = -1.0 * (p >= f)
    cb[:, CB_NONES:CB_NONES + 128] = -1.0
    cb[:, CB_MASK:CB_MASK + 128] = (p < f)
    cb[:, CB_ONESD:CB_ONESD + 128] = 1.0 / 128.0
    p64 = np.arange(64)[:, None]
    f64 = np.arange(64)[None, :]
    same = (p64 // 16) == (f64 // 16)
    cb[:64, CB_SMASK:CB_SMASK + 64] = same & ((p64 % 16) < (f64 % 16))
    cb[:64, CB_SNTRI:CB_SNTRI + 64] = -1.0 * (same & (p64 >= f64))
    cf = np.zeros((128, CF_W), np.float32)
    lg = np.log(1.0 - 2.0 ** (-5.0 - np.arange(NH, dtype=np.float64)))
    for h in range(NH):
        j = np.arange(128)[:, None]
        i = np.arange(128)[None, :]
        samechunk = (j // 64) == (i // 64)
        earlier = (j // 64) < (i // 64)
        m = np.where(samechunk, np.exp(lg[h] * np.abs(i - j)), np.where(earlier, np.exp(lg[h] * (i - j)), 0.0))
        cf[:, CF_DM + h * 128:CF_DM + (h + 1) * 128] = m
        ms = np.where(same, np.exp(lg[h] * np.abs((f64 % 16) - (p64 % 16))), 0.0)
        cf[:64, CF_DSM + h * 64:CF_DSM + (h + 1) * 64] = ms
        cf[:, CF_DW + h] = np.exp(lg[h] * (127.0 - np.arange(128)))
        cf[:, CF_DR + h] = np.exp(lg[h] * (np.arange(128) + 1.0))
        cf[:64, CF_DWS + h] = np.exp(lg[h] * (15.0 - (np.arange(64) % 16)))
        cf[:64, CF_DRS + h] = np.exp(lg[h] * ((np.arange(64) % 16) + 1.0))
    for s in range(4):
        cf[:64, CF_MSK + s] = (np.arange(64) // 16 == s)
    half = 64
    inv = (10000.0 ** (-np.arange(half, dtype=np.float32) / half)).astype(np.float32)
    rope = np.zeros((NTT + 1, 128, 256), np.float32)
    for tt in range(NTT + 1):
        if tt < NTT:
            pos = (tt * 128 + np.arange(128)).astype(np.float32)
        else:
            pos = (T + (np.arange(128) % 16)).astype(np.float32)
        ang = (pos[:, None] * inv[None, :]).astype(np.float32)
        c = np.cos(ang.astype(np.float64))
        s = np.sin(ang.astype(np.float64))
        rope[tt, :, 0:64] = c
        rope[tt, :, 64:128] = c * QSCALE
        rope[tt, :, 128:192] = s
        rope[tt, :, 192:256] = s * QSCALE
    gam = np.exp(lg)
    return cb, cf, rope, gam


class Builder:
    def __init__(self):
        nc = bass.Bass("TRN2", target_bir_lowering=False)
        self.nc = nc
        self.gam = _consts()[3]
        di = lambda n, s: nc.dram_tensor(n, s, F32, kind="ExternalInput").ap()
        do = lambda n, s: nc.dram_tensor(n, s, F32, kind="ExternalOutput").ap()
        self.xp = di("xp", [2, T, D])
        self.xs = di("xs", [NS, D])
        self.ck = di("ck", [4, T, NH * HD])
        self.cv = di("cv", [4, T, NH * HD])
        self.sr = di("sr", [4, NH, HD, HD])
        self.w_in = di("w_in", [D, 8192])
        self.w_out = di("w_out", [D, D])
        self.g1T_d = di("g1T", [128, NDC])
        self.gsb_d = di("gsb", [128, NH])
        self.gr_d = di("gr", [128, NH])
        self.gpost_d = di("gpost", [128, D])
        self.cb_d = di("cb", [128, CB_W])
        self.cf_d = di("cf", [128, CF_W])
        self.rope_d = di("rope", [NTT + 1, 128, 256])
        self.yp = do("yp", [2, T, D])
        self.ys = do("ys", [NS, D])
        self.kp = do("kp", [2, T, NH * HD])
        self.vp = do("vp", [2, T, NH * HD])
        self.spo = do("spo", [2, NH, HD, HD])
        self.ks = do("ks", [NS, NH * HD])
        self.vs = do("vs", [NS, NH * HD])
        self.ss = do("ss", [4, NH, HD, HD])
        self.wob = nc.dram_tensor("wob", [D, D], BF16, kind="Internal").ap()
        self.wob_t = TT(self.wob)
        self._wo_loaded = False
        import os as _os
        self.dbg = do("dbg", [128, 4096]) if _os.environ.get("KDEV_DBG") else None
        self._dsem_dbg = None

        self.pe = Eng(nc, "pe", nc.tensor)
        self.act = Eng(nc, "act", nc.scalar)
        self.dve = Eng(nc, "dve", nc.vector)
        self.pool = Eng(nc, "pool", nc.gpsimd)
        self.sp = Eng(nc, "sp", nc.sync)
        self.engs = [self.pe, self.act, self.dve, self.pool, self.sp]
        self.dsems = []
        self.out_toks = []

        total = (nc.sbuf_bytes_remaining // 32) * 32 - 512
        self.ar = Arena(nc, total)
        self.pb = [TT(nc.alloc_psum_tensor(f"pb{i}", [128, 512], F32)[:], Trk(excl=True)) for i in range(8)]
        self._rr = {}

    def dsem(self, name):
        s = Sem(self.nc.alloc_semaphore("d_" + name))
        self.dsems.append(s)
        return s

    def barrier(self):
        for e in self.engs:
            for x in self.engs:
                if x is not e and x.sem.val > 0:
                    e.wait(Tok(x.sem, x.sem.val))
            for d in self.dsems:
                if d.val > 0:
                    e.wait(Tok(d, d.val))

    def bank(self, group, ids):
        i = self._rr.get(group, 0)
        self._rr[group] = i + 1
        return self.pb[ids[i % len(ids)]]

    def mm(self, out, lhsT, rhs, start=True, stop=True, signal=True, predeps=True, reads=None, writes=None, skip=False):
        rd = reads if reads is not None else [lhsT, rhs]
        wr = writes if writes is not None else [out]
        kw = {}
        if skip:
            kw["skip_group_check"] = True
        return self.pe.op(lambda h: h.matmul(out.ap, lhsT=lhsT.ap, rhs=rhs.ap, start=start, stop=stop, **kw),
                          rd, wr, signal=signal, predeps=predeps)

    def mm_group(self, items, reads, writes, skip=False):
        self.pe.deps(reads, writes)
        tok = None
        n = len(items)
        for k, (o, l, r, st, sp_) in enumerate(items):
            tok = self.mm(o, l, r, start=st, stop=sp_, signal=(k == n - 1), predeps=False,
                          reads=reads, writes=writes, skip=skip)
        return tok

    def mm_group_gen(self, items, reads, writes, chunk=4, skip=False):
        self.pe.deps(reads, writes)
        n = len(items)
        for k, (o, l, r, st, sp_) in enumerate(items):
            self.mm(o, l, r, start=st, stop=sp_, signal=(k == n - 1), predeps=False,
                    reads=reads, writes=writes, skip=skip)
            if (k + 1) % chunk == 0 and k + 1 < n:
                yield

    def transpose(self, out, in_, ident, signal=True):
        return self.pe.op(lambda h: h.transpose(out.ap, in_.ap, ident.ap), [in_, ident], [out], signal=signal)

    def actf(self, out, in_, func, scale=1.0, bias=0.0, accum=None, extra_reads=()):
        kw = {}
        if accum is not None:
            kw["accum_out"] = accum.ap
        sc = scale.ap if isinstance(scale, TT) else scale
        rd = [in_] + list(extra_reads)
        if isinstance(scale, TT):
            rd.append(scale)
        wr = [out] + ([accum] if accum is not None else [])
        return self.act.op(lambda h: h.activation(out=out.ap, in_=in_.ap, func=func, bias=bias, scale=sc, **kw), rd, wr)

    def tt(self, eng, out, in0, in1, op):
        return eng.op(lambda h: h.tensor_tensor(out=out.ap, in0=in0.ap, in1=in1.ap, op=op), [in0, in1], [out])

    def ts(self, eng, out, in0, s1, op0, s2=None, op1=None):
        rd = [in0]
        a1 = s1
        a2 = s2
        if isinstance(s1, TT):
            rd.append(s1)
            a1 = s1.ap
        if isinstance(s2, TT):
            rd.append(s2)
            a2 = s2.ap
        if op1 is None:
            return eng.op(lambda h: h.tensor_scalar(out=out.ap, in0=in0.ap, scalar1=a1, scalar2=None, op0=op0), rd, [out])
        return eng.op(lambda h: h.tensor_scalar(out=out.ap, in0=in0.ap, scalar1=a1, scalar2=a2, op0=op0, op1=op1), rd, [out])

    def stt(self, out, in0, scalar, in1, op0, op1):
        rd = [in0, in1]
        a = scalar
        if isinstance(scalar, TT):
            rd.append(scalar)
            a = scalar.ap
        return self.dve.op(lambda h: h.scalar_tensor_tensor(out=out.ap, in0=in0.ap, scalar=a, in1=in1.ap, op0=op0, op1=op1),
                           rd, [out])

    def copy(self, eng, out, in_):
        if eng is self.act:
            return eng.op(lambda h: h.copy(out=out.ap, in_=in_.ap), [in_], [out])
        return eng.op(lambda h: h.tensor_copy(out=out.ap, in_=in_.ap), [in_], [out])

    def memset(self, eng, out, val):
        return eng.op(lambda h: h.memset(out.ap, val), [], [out])

    def store(self, dram_ap, src, dsem):
        tok = self.sp.dma(dram_ap, src.ap, dsem, [src], [])
        self.out_toks.append(tok)
        return tok

    def dump(self, src, col0):
        if self.dbg is None:
            return
        if self._dsem_dbg is None:
            self._dsem_dbg = self.dsem("dbg")
        p, n = src.ap.shape[0], src.ap.shape[1]
        tok = self.pool.dma(self.dbg[0:p, col0:col0 + n], src.ap, self._dsem_dbg, [src], [])
        self.out_toks.append(tok)

    def setup(self):
        ar = self.ar
        self.hT = ar.alloc([NDC, TS], BF16)
        self.yT = ar.alloc([NDC, TS], BF16)
        self.wring = [ar.alloc([NDC, 512], BF16) for _ in range(2)]
        self.wsem = [self.dsem(f"w{i}") for i in range(2)]
        self.cb = ar.alloc([CB_W], BF16)
        self.cf = ar.alloc([CF_W - 1536], F32)
        self.gains = ar.alloc([NDC + 2 * NH], F32)
        self.sq_q = ar.alloc([NH, NS], BF16)
        self.sq_k = ar.alloc([NH, NS], BF16)
        self.sq_g = ar.alloc([NH, NS], BF16)
        self.sq_v = ar.alloc([NH, HD], BF16)
        self.csem = self.dsem("const")
        m = ar.mark()
        cbf = ar.alloc([CB_W], F32)
        sp = self.sp
        sp.dma(cbf.ap, self.cb_d, self.csem, [], [cbf])
        sp.dma(self.cf.ap, self.cf_d[:, 1536:CF_W], self.csem, [], [self.cf])
        sp.dma(self.gains.ap[:, 0:NDC], self.g1T_d, self.csem, [], [self.gains])
        sp.dma(self.gains.ap[:, NDC:NDC + NH], self.gsb_d, self.csem, [], [self.gains])
        sp.dma(self.gains.ap[:, NDC + NH:NDC + 2 * NH], self.gr_d, self.csem, [], [self.gains])
        final = Tok(self.csem, self.csem.val)
        for t in (cbf, self.cf, self.gains):
            t.trk.w = final
        self.copy(self.dve, self.cb, cbf)
        self.barrier()
        ar.release(m)
        cb = self.cb
        self.ident = cb[:, CB_IDENT:CB_IDENT + 128]
        self.ntri = cb[:, CB_NTRI:CB_NTRI + 128]
        self.nones = cb[:, CB_NONES:CB_NONES + 128]
        self.mask = cb[:, CB_MASK:CB_MASK + 128]
        self.onesd = cb[:, CB_ONESD:CB_ONESD + 128]
        self.smask = cb[0:64, CB_SMASK:CB_SMASK + 64]
        self.sntri = cb[0:64, CB_SNTRI:CB_SNTRI + 64]
        self.work_mark = ar.mark()

    def precast_wout(self, part):
        if not hasattr(self, "_wobsem"):
            self._wobsem = self.dsem("wob")
        src = self.w_out.rearrange("(p r) (c n) -> p (r c) n", p=128, n=1024)
        dst = self.wob.rearrange("(p r) (c n) -> p (r c) n", p=128, n=1024)
        self.pool.h.dma_start(out=dst[:, part * 8:(part + 1) * 8, :], in_=src[:, part * 8:(part + 1) * 8, :]).then_inc(self._wobsem.h, 16)
        self._wobsem.val += 16
        self.wob_t.trk.w = Tok(self._wobsem, self._wobsem.val)

    def wout_view(self):
        return TT(self.hT.ap.rearrange("p a b -> p (a b)")[:, 0:NDC * D].rearrange("p (a b) -> p a b", a=NDC), self.hT.trk)

    def issue_wout_load(self):
        if not hasattr(self, "_wosem"):
            self._wosem = self.dsem("wo")
        Wo = self.wout_view()
        wv = self.wob.rearrange("(c p) n -> p c n", p=128)
        self.sp.deps([self.wob_t], [Wo])
        for c in range(2):
            self.sp.h.dma_start(out=Wo.ap[:, :, c * 1024:(c + 1) * 1024], in_=wv[:, :, c * 1024:(c + 1) * 1024]).then_inc(self._wosem.h, 16)
            self._wosem.val += 16
        self.sp.done(Tok(self._wosem, self._wosem.val), [self.wob_t], [Wo])
        self._wo_loaded = True

    def load_wgroup(self, gi, colblocks):
        slot = self.wring[gi % 2]
        sem = self.wsem[gi % 2]
        wv = self.w_in.rearrange("(c p) n -> p c n", p=128)
        self.pool.deps([], [slot])
        tok = None
        for k, c0 in enumerate(colblocks):
            self.pool.h.dma_start(out=slot.ap[:, :, k * 128:(k + 1) * 128], in_=wv[:, :, c0:c0 + 128]).then_inc(sem.h, 16)
            sem.val += 16
        tok = Tok(sem, sem.val)
        self.pool.done(tok, [], [slot])
        return slot

    def phase_a(self, seq, with_sample):
        ar = self.ar
        m = ar.mark()
        xs_ = [ar.alloc([D], F32) for _ in range(3)]
        xsem = [self.dsem(f"xa{i}") for i in range(3)] if not hasattr(self, "_xsem") else self._xsem
        self._xsem = xsem
        junk = ar.alloc([D], BF16)
        xna = [ar.alloc([D // 2], BF16) for _ in range(2)]
        xnb = [ar.alloc([D // 2], BF16) for _ in range(2)]
        stats = [ar.alloc([8], F32) for _ in range(2)]
        ntiles = NTT + (1 if with_sample else 0)
        g1 = self.gains

        def load(tt):
            x_t = xs_[tt % 3]
            if tt < NTT:
                self.sp.dma(x_t.ap, self.xp[seq, tt * 128:(tt + 1) * 128, :], xsem[tt % 3], [], [x_t])
            else:
                self.sp.dma(x_t.ap[0:NS, :], self.xs[:, :], xsem[tt % 3], [], [x_t])

        load(0)
        load(1)
        for tt in range(ntiles):
            if tt + 2 < ntiles:
                load(tt + 2)
            n = 128 if tt < NTT else NS
            t0 = tt * 128
            x_t = xs_[tt % 3][0:n]
            st = stats[tt % 2][0:n]
            self.actf(junk[0:n], x_t, AF.Square, accum=st[:, 0:1])
            self.actf(st[:, 1:2], st[:, 0:1], AF.Ln, scale=1.0 / D, bias=EPS)
            self.actf(st[:, 2:3], st[:, 1:2], AF.Exp, scale=-0.5)
            xh = [xna[tt % 2][0:n], xnb[tt % 2][0:n]]
            self.ts(self.dve, xh[0], x_t[:, 0:D // 2], st[:, 2:3], ALU.mult)
            self.actf(xh[1], x_t[:, D // 2:D], AF.Copy, scale=st[:, 2:3])
            for half in range(2):
                bk = self.bank("a", [0, 1, 2, 3])
                bb = bk.bitcast(BF16)
                for k in range(8):
                    self.transpose(bb[:, k * 128:k * 128 + n], xh[half][:, k * 128:(k + 1) * 128],
                                   self.ident[0:n, 0:n], signal=(k == 7))
                src = bb[:, 0:1024].rearrange("p (a b) -> p a b", a=8)[:, :, 0:n]
                gb = TT(g1.ap[:, half * 8:half * 8 + 8].unsqueeze(2).broadcast_to([128, 8, n]), g1.trk)
                self.tt(self.dve, self.hT[:, half * 8:half * 8 + 8, t0:t0 + n], src, gb, ALU.mult)
        self.barrier()
        ar.release(m)

    def silu_evac(self, bank_v, out, tmp):
        self.actf(tmp, bank_v, AF.Exp, scale=-1.0)
        self.actf(tmp, tmp, AF.Ln, bias=1.0)
        self.actf(tmp, tmp, AF.Exp, scale=-1.0)
        self.tt(self.dve, out, bank_v, tmp, ALU.mult)

    def epilogue(self, O, n, sg, gain, yout, sq, rr, tf, defer=None, ms_ids=(6, 7), defer_mm=False, copy_o=None):
        self.actf(sq[:, 0:n], O, AF.Square)
        if copy_o is None:
            copy_o = defer_mm
        if copy_o:
            self.copy(self.dve, tf[:, 0:n], O)
        box = {}

        def a2():
            box["ms"] = self.bank("ms" + str(ms_ids), list(ms_ids))
            self.mm(box["ms"][:, 0:n], self.onesd, sq[:, 0:n])

        if not defer_mm:
            a2()

        class _L:
            def __getitem__(s_, idx):
                return box["ms"][idx]
        ms = _L()

        def b1():
            self.actf(rr[:, 0:n], ms[:, 0:n], AF.Ln, bias=EPS)

        def b2():
            self.actf(rr[:, 0:n], rr[:, 0:n], AF.Exp, scale=-0.5)

        def b3():
            self.tt(self.dve, tf[:, 0:n], (tf[:, 0:n] if copy_o else O), rr[:, 0:n], ALU.mult)
            self.stt(yout, tf[:, 0:n], gain, sg, ALU.mult, ALU.mult)

        if defer is None:
            b1()
            b2()
            b3()
        else:
            if defer_mm:
                a2.is_a2 = True
                defer.append(a2)
            defer.extend([b1, b2, b3])

    def phase_c(self, seq, with_sample, gi0):
        ar = self.ar
        m = ar.mark()
        NPAR = 2
        qTb = [ar.alloc([512], BF16) for _ in range(4)]
        sgb = [ar.alloc([512], BF16) for _ in range(4)]
        kT123 = [ar.alloc([512], BF16) for _ in range(3)]
        V123 = [ar.alloc([4, HD], BF16) for _ in range(3)]
        kTb = [[ar.alloc([512], BF16)] + kT123 for _ in range(NPAR)]
        Vhb = [[ar.alloc([4, HD], BF16)] + V123 for _ in range(NPAR)]
        e1 = ar.alloc([512], F32)
        sp_ = [ar.alloc([512], BF16) for _ in range(2)]
        a_ = [ar.alloc([512], BF16) for _ in range(3)]
        R = ar.alloc([512], BF16)
        sq = ar.alloc([512], BF16)
        rr = ar.alloc([512], F32)
        stages = [ar.alloc([2, 256], F32) for _ in range(2)]
        kbf = ar.alloc([4, HD], BF16)
        self._stg = 0
        tf = ar.alloc([512], F32)
        tg = rr
        if not hasattr(self, "_stsem"):
            self._stsem = [self.dsem("stage0"), self.dsem("stage1")]
        hT = self.hT
        gsb = self.gains[:, NDC:NDC + NH]
        PB = [6, 7]

        def gen_inproj(h, W, tb):
            par = h % NPAR
            samp = tb == 4
            t0, n = (T, NS) if samp else (tb * 512, 512)

            def fm_group(ci, kind):
                bk = self.bank("cproj", PB)
                items = [(bk[:, 0:n], W[:, dc, ci * 128:(ci + 1) * 128], hT[:, dc, t0:t0 + n], dc == 0, dc == NDC - 1)
                         for dc in range(NDC)]
                yield from self.mm_group_gen(items, [W, hT], [bk])
                if kind == "q":
                    dst = self.sq_q[:, h, :] if samp else qTb[tb]
                    self.ts(self.dve, dst, bk[:, 0:n], QSCALE, ALU.mult)
                elif kind == "k":
                    dst = self.sq_k[:, h, :] if samp else kTb[par][tb]
                    self.copy(self.dve, dst, bk[:, 0:n])
                else:
                    dst = self.sq_g[:, h, :] if samp else sgb[tb]
                    tmp = tg[:, 0:n]
                    self.actf(tmp, bk[:, 0:n], AF.Exp, scale=-1.0)
                    yield
                    self.actf(tmp, tmp, AF.Ln, bias=1.0)
                    yield
                    self.actf(tmp, tmp, AF.Exp, scale=-1.0)
                    self.tt(self.dve, dst, bk[:, 0:n], tmp, ALU.mult)
                yield

            if not samp:
                for u2 in range(2):
                    bk = self.bank("cproj", PB)
                    items = []
                    for u in range(2):
                        tt_ = tb * 4 + u2 * 2 + u
                        for dc in range(NDC):
                            items.append((bk[:, u * 256:(u + 1) * 256], hT[:, dc, tt_ * 128:(tt_ + 1) * 128],
                                          W[:, dc, 128:384], dc == 0, dc == NDC - 1))
                    yield from self.mm_group_gen(items, [W, hT], [bk])
                    bv = bk.rearrange("p (a b) -> p a b", a=2)
                    stage = stages[self._stg % 2]
                    stsem = self._stsem[self._stg % 2]
                    self._stg += 1
                    self.copy(self.dve, stage, bv)
                    self.copy(self.dve, Vhb[par][tb][:, u2 * 2:u2 * 2 + 2, :], bv[:, :, 128:256])
                    self.copy(self.dve, kbf[:, u2 * 2:u2 * 2 + 2, :], bv[:, :, 0:128])
                    r0 = (tb * 4 + u2 * 2) * 128
                    kdst = self.kp[seq, r0:r0 + 256, h * 128:(h + 1) * 128].rearrange("(a p) c -> p a c", p=128)
                    vdst = self.vp[seq, r0:r0 + 256, h * 128:(h + 1) * 128].rearrange("(a p) c -> p a c", p=128)
                    self.store(kdst, stage[:, :, 0:128], stsem)
                    self.store(vdst, stage[:, :, 128:256], stsem)
                    yield
                yield from fm_group(0, "q")
                TBk = self.bank("cproj", PB).bitcast(BF16)
                for u in range(4):
                    self.transpose(TBk[:, u * 128:(u + 1) * 128], kbf[:, u, :], self.ident, signal=(u == 3))
                self.copy(self.dve, kTb[par][tb], TBk[:, 0:512])
                yield
                yield from fm_group(3, "g")
            else:
                yield from fm_group(0, "q")
                yield from fm_group(1, "k")
                yield from fm_group(3, "g")
                bk = self.bank("cproj", PB)
                items = [(bk[0:NS, 0:256], hT[:, dc, T:T + NS], W[:, dc, 128:384], dc == 0, dc == NDC - 1)
                         for dc in range(NDC)]
                yield from self.mm_group_gen(items, [W, hT], [bk])
                stage = stages[self._stg % 2]
                stsem = self._stsem[self._stg % 2]
                self._stg += 1
                self.copy(self.dve, stage[0:NS, 0, :], bk[0:NS, 0:256])
                self.copy(self.dve, self.sq_v[0:NS, h, :], bk[0:NS, 128:256])
                self.store(self.ks[:, h * 128:(h + 1) * 128], stage[0:NS, 0, 0:128], stsem)
                self.store(self.vs[:, h * 128:(h + 1) * 128], stage[0:NS, 0, 128:256], stsem)
                yield

        def gen_attn(h, I):
            par = h % NPAR
            O = self.bank("cO", [3, 4])
            self.memset(self.dve, O, 0.0)
            self.memset(self.dve, R, 0.0)
            js = list(range(4 * I + 3, -1, -1))
            pend2 = None
            pend3 = None
            pend3n = None

            def geom(j):
                jj = j - 4 * I
                return (128 * jj if jj > 0 else 0), jj >= 0, 128 * jj

            def issue_z(j):
                c0, _, _ = geom(j)
                Z = self.bank("cZ", [0, 1, 2])
                kT_j = kTb[par][j // 4][:, (j % 4) * 128:(j % 4 + 1) * 128]
                self.mm(Z[:, c0:512], kT_j, qTb[I][:, c0:512], skip=True)
                return Z

            Znext = issue_z(js[0])
            for bi, j in enumerate(js):
                c0, diag, dcol = geom(j)
                Z = Znext
                if bi + 1 < len(js):
                    Znext = issue_z(js[bi + 1])
                e = e1
                spb = sp_[bi % 2]
                ab = a_[bi % 3]
                V_j = Vhb[par][j // 4][:, j % 4, :]
                self.actf(e[:, c0:512], Z[:, c0:512], AF.Exp)
                self.actf(spb[:, c0:512], e[:, c0:512], AF.Ln, bias=1.0)
                if diag:
                    self.tt(self.dve, spb[:, dcol:dcol + 128], spb[:, dcol:dcol + 128], self.mask, ALU.mult)

                def stage2(Z=Z, spb=spb, ab=ab, c0=c0, diag=diag, dcol=dcol, first=(bi == 0), last=(j == 0)):
                    items = [(Z[:, c0:512], self.ntri, spb[:, c0:512], False, first)]
                    rd = [self.cb, spb]
                    if not first:
                        items.append((Z[:, c0:512], self.nones, R[:, c0:512], False, True))
                        rd.append(R)
                    self.mm_group(items, rd, [Z], skip=True)
                    if not last:
                        self.tt(self.dve, R[:, c0:512], R[:, c0:512], spb[:, c0:512], ALU.add)
                    self.actf(ab[:, c0:512], Z[:, c0:512], AF.Exp)
                    if diag:
                        self.tt(self.dve, ab[:, dcol:dcol + 128], ab[:, dcol:dcol + 128], self.mask, ALU.mult)

                def stage3(ab=ab, c0=c0, V_j=V_j, last=(j == 0)):
                    self.mm(O[:, c0:512], V_j, ab[:, c0:512], start=False, stop=last, skip=True)

                if pend2 is not None:
                    pend2()
                if pend3 is not None:
                    pend3()
                if bi >= 1 and cdef:
                    cdef.pop(0)()
                pend3 = pend3n
                pend3n = stage3
                pend2 = stage2
                yield
            while cdef:
                cdef.pop(0)()
            pend2()
            if pend3 is not None:
                pend3()
            pend3n()
            self.epilogue(O, 512, sgb[I], gsb[:, h:h + 1], self.yT[:, h, I * 512:(I + 1) * 512], sq, rr, tf, defer=cdef, ms_ids=(5,), defer_mm=True, copy_o=False)
            yield

        cdef = []

        def interleave(main, nmain, fillers, nfill):
            fl = [f for f in fillers if f is not None]
            done_f = 0
            done_m = 0

            def fill_one():
                while fl:
                    try:
                        next(fl[0])
                        return True
                    except StopIteration:
                        fl.pop(0)
                return False

            for _ in main:
                done_m += 1
                while nfill and done_f * nmain < done_m * nfill:
                    if not fill_one():
                        break
                    done_f += 1
            while fill_one():
                pass

        Wh = {0: (self._wpre.pop(gi0) if gi0 in getattr(self, '_wpre', {}) else self.load_wgroup(gi0, self._sb_cols(0)))}
        for _ in gen_inproj(0, Wh[0], 0):
            pass
        for h in range(NH):
            Wn_cols = self._sb_cols(h + 1) if h + 1 < NH else self._ret_cols(0)
            Wh[h + 1] = self.load_wgroup(gi0 + h + 1, Wn_cols)
            if seq == 0 and 1 <= h <= 4:
                self.precast_wout(h - 1)
            W = Wh[h]
            for I in range(4):
                nblk = 4 * I + 5
                if I < 3:
                    fillers = [gen_inproj(h, W, I + 1)]
                    nfill = 27
                else:
                    fillers = []
                    nfill = 0
                    if with_sample:
                        fillers.append(gen_inproj(h, W, 4))
                        nfill += 18
                    if h + 1 < NH:
                        fillers.append(gen_inproj(h + 1, Wh[h + 1], 0))
                        nfill += 27
                interleave(gen_attn(h, I), nblk, fillers, nfill)
        while cdef:
            cdef.pop(0)()
        W = Wh[NH]
        self.barrier()
        ar.release(m)
        return W

    def _sb_cols(self, h):
        return [0 * 1024 + h * 128, 1 * 1024 + h * 128, 2 * 1024 + h * 128, 3 * 1024 + h * 128]

    def _ret_cols(self, h):
        return [4 * 1024 + h * 128, 5 * 1024 + h * 128, 6 * 1024 + h * 128, 7 * 1024 + h * 128]

    def phase_d(self, seq, with_sample, gi0, W0, next_cols):
        ar = self.ar
        m = ar.mark()
        hT = self.hT
        cfd = ar.alloc([1536], F32)
        if not hasattr(self, "_rtsem"):
            self._rtsem = [self.dsem(f"rt{i}") for i in range(3)]
            self._sosem = [self.dsem(f"so{i}") for i in range(2)]
            self._sinsem = self.dsem("sin")
            self._snsem = self.dsem("snew")
            self._cfdsem = self.dsem("cfd")
        self.sp.dma(cfd.ap, self.cf_d[:, CF_DM:CF_DM + 1536], self._cfdsem, [], [cfd])
        sgb = [ar.alloc([512], BF16) for _ in range(4)] + [ar.alloc([NS], BF16)]
        tf = ar.alloc([512], F32)
        sq = ar.alloc([512], BF16)
        rr = ar.alloc([512], F32)
        tg = rr
        ra = [ar.alloc([4, 2, 64], F32) for _ in range(2)]
        qk = [ar.alloc([2, HD], BF16) for _ in range(4)]
        vt = [ar.alloc([HD], BF16) for _ in range(5)]
        qd = [ar.alloc([HD], BF16) for _ in range(4)]
        kd = [ar.alloc([HD], BF16) for _ in range(5)]
        qkT = [ar.alloc([3 * HD], BF16) for _ in range(3)]
        scm = [ar.alloc([HD], BF16) for _ in range(3)]
        S32 = [ar.alloc([HD], F32) for _ in range(2)]
        Sbf = [ar.alloc([HD], BF16) for _ in range(2)]
        rt = [ar.alloc([2, 2, 64], F32) for _ in range(2)]
        rtsem = self._rtsem
        if with_sample:
            Sin32 = ar.alloc([4, HD], F32)
            Sinb = ar.alloc([4, HD], BF16)
            Snew = Sin32
            kdm = ar.alloc([4, HD], BF16)
            tfs = ar.alloc([NS], F32)
        gr = self.gains[:, NDC + NH:NDC + 2 * NH]
        cf = self.cf
        ntiles = NTT + (1 if with_sample else 0)
        tblocks = [(i * 512, 512) for i in range(4)] + ([(T, NS)] if with_sample else [])
        steps = [(h, t) for h in range(NH) for t in range(ntiles)]
        Wh = {0: W0}
        st = {}

        def gate_block(h, b):
            t0, n = tblocks[b]
            W = Wh[h]
            bk = self.bank("dproj", [5])
            items = [(bk[:, 0:n], W[:, dc, 384:512], hT[:, dc, t0:t0 + n], dc == 0, dc == NDC - 1) for dc in range(NDC)]
            self.mm_group(items, [W, hT], [bk])
            self.silu_evac(bk[:, 0:n], sgb[b][:, 0:n], tg[:, 0:n])

        def S1(k):
            h, tt_ = steps[k]
            W = Wh[h]
            if tt_ == 0:
                if h + 1 < NH:
                    Wh[h + 1] = self.load_wgroup(gi0 + h + 1, self._ret_cols(h + 1))
                elif next_cols is not None:
                    Wh[h + 1] = self.load_wgroup(gi0 + h + 1, next_cols)
            if with_sample and tt_ == 8:
                self.sp.dma(Sin32.ap, self.sr[:, h, :, :].rearrange("s d e -> d s e"), self._sinsem, [], [Sin32])
                self.copy(self.act, Sinb, Sin32)
            n = 128 if tt_ < NTT else NS
            t0 = tt_ * 128
            samp = tt_ >= NTT
            rtab = rt[k % 2]
            self.sp.dma(rtab.ap, self.rope_d[tt_].rearrange("p (a b c) -> p a b c", a=2, b=2), rtsem[k % 2], [], [rtab])
            P = self.bank("dP", [0, 1])
            items = [(P[0:n, 0:384], hT[:, dc, t0:t0 + n], W[:, dc, 0:384], dc == 0, dc == NDC - 1) for dc in range(NDC)]
            self.mm_group(items, [W, hT], [P])
            st[k] = dict(P=P, rtab=rtab, n=n, samp=samp, t0=t0)

        def S1b(k):
            h, tt_ = steps[k]
            d0 = st[k]
            P, rtab, n, samp, t0 = d0["P"], d0["rtab"], d0["n"], d0["samp"], d0["t0"]
            Pv = P[0:n, 0:256].rearrange("p (a b c) -> p a b c", a=2, b=2)
            X1 = Pv[:, :, 0, :]
            X2 = Pv[:, :, 1, :]
            Ct = rtab[0:n, 0]
            St = rtab[0:n, 1]
            r_ = ra[k % 2][0:n]
            tabv = rtab[0:n]
            r4 = TT(r_.ap.rearrange("p (u t) b c -> p u t b c", u=2), r_.trk)
            for u, X in ((0, X1), (1, X2)):
                xb_ = TT(X.ap.unsqueeze(1).broadcast_to([n, 2, 2, 64]), X.trk)
                self.tt(self.dve, r4[:, u], xb_, tabv, ALU.mult)
            qk_ = qk[k % 4][0:n]
            self.tt(self.dve, qk_[:, :, 0:64], r_[:, 0], r_[:, 3], ALU.subtract)
            self.tt(self.dve, qk_[:, :, 64:128], r_[:, 1], r_[:, 2], ALU.add)
            st[k]["qk"] = qk_

        def S1c(k):
            h, tt_ = steps[k]
            d0 = st[k]
            P, n, samp, qk_ = d0["P"], d0["n"], d0["samp"], d0["qk"]
            v_ = vt[k % 5][0:n]
            self.copy(self.act, v_, P[0:n, 256:384])
            qd_ = qd[k % 4][0:n]
            kd_ = kd[k % 5][0:n]
            if not samp:
                self.actf(qd_, qk_[:, 0, :], AF.Copy, scale=cf[0:n, (CF_DR - 1536) + h:(CF_DR - 1536) + h + 1])
                self.actf(kd_, qk_[:, 1, :], AF.Copy, scale=cf[0:n, (CF_DW - 1536) + h:(CF_DW - 1536) + h + 1])
            else:
                self.actf(qd_, qk_[:, 0, :], AF.Copy, scale=cf[0:n, (CF_DRS - 1536) + h:(CF_DRS - 1536) + h + 1])
                for s in range(4):
                    self.ts(self.dve, kdm[0:n, s, :], qk_[:, 1, :], cf[0:n, (CF_DWS - 1536) + h:(CF_DWS - 1536) + h + 1], ALU.mult,
                            cf[0:n, (CF_MSK - 1536) + s:(CF_MSK - 1536) + s + 1], ALU.mult)
            st[k].update(dict(v=v_, qk=qk_, qd=qd_, kd=kd_))
            if tt_ < len(tblocks):
                gate_block(h, tt_)

        def S2(k):
            h, tt_ = steps[k]
            d = st[k]
            n, samp = d["n"], d["samp"]
            TB = self.bank("dT", [2]).bitcast(BF16)
            idn = self.ident[0:n, 0:n]
            self.transpose(TB[:, 0:n], d["qk"][:, 0, :], idn, signal=False)
            self.transpose(TB[:, n:2 * n], d["qd"], idn, signal=False)
            self.transpose(TB[:, 2 * n:3 * n], d["qk"][:, 1, :], idn, signal=True)
            qkT_ = qkT[k % 3]
            self.copy(self.act, qkT_[:, 0:3 * n], TB[:, 0:3 * n])
            d["qkT"] = qkT_

        def S2b(k):
            h, tt_ = steps[k]
            d = st[k]
            n, samp = d["n"], d["samp"]
            qkT_ = d["qkT"]
            SC = self.bank("dS", [3])
            self.mm(SC[0:n, 0:n], qkT_[:, 2 * n:3 * n], qkT_[:, 0:n])
            scm_ = scm[k % 3][0:n, 0:n]
            if not samp:
                self.tt(self.dve, scm_, SC[0:n, 0:n], cfd[:, h * 128:(h + 1) * 128], ALU.mult)
            else:
                self.tt(self.dve, scm_, SC[0:n, 0:n], cfd[0:n, 1024 + h * 64:1024 + (h + 1) * 64], ALU.mult)
            d["qkT"] = qkT_
            d["scm"] = scm_

        def S3(k):
            h, tt_ = steps[k]
            d = st.pop(k)
            n, samp = d["n"], d["samp"]
            v_, scm_, qkT_, kd_ = d["v"], d["scm"], d["qkT"], d["kd"]
            qdT_ = qkT_[:, n:2 * n]
            S = S32[h % 2]
            if not samp:
                if tt_ % 4 == 0:
                    self._dO = self.bank("dO", [4])
                O = self._dO
                Oc = O[:, (tt_ % 4) * 128:(tt_ % 4 + 1) * 128]
                items = [(Oc, v_, scm_, True, tt_ == 0)]
                rd = [v_, scm_]
                if tt_ > 0:
                    items.append((Oc, Sbf[(tt_ - 1) % 2], qdT_, False, True))
                    rd += [Sbf[(tt_ - 1) % 2], qkT_]
                self.mm_group(items, rd, [O])
                U = self.bank("dU", [7])
                self.mm(U[:, 0:128], kd_, v_)
                if tt_ == 0:
                    self.copy(self.dve, S, U[:, 0:128])
                else:
                    self.stt(S, S, float(self.gam[h] ** 128), U[:, 0:128], ALU.mult, ALU.add)
                if tt_ < NTT - 1:
                    self.copy(self.act, Sbf[tt_ % 2], S)
                else:
                    self.store(self.spo[seq, h, :, :], S, self._sosem[h % 2])
                if tt_ % 4 == 3:
                    I = tt_ // 4
                    self.epilogue(O, 512, sgb[I], gr[:, h:h + 1], self.yT[:, NH + h, I * 512:(I + 1) * 512], sq, rr, tf, defer=deferred, ms_ids=(6,), defer_mm=True)
            else:
                Os = self.bank("dO", [4])
                items = [(Os[:, 0:NS], v_, scm_, True, False)]
                for s in range(4):
                    items.append((Os[:, s * 16:(s + 1) * 16], Sinb[:, s, :], qdT_[:, s * 16:(s + 1) * 16], False, s == 3))
                self.mm_group(items, [v_, scm_, Sinb, qkT_], [Os], skip=True)
                U = self.bank("dU", [7])
                items = [(U[:, s * 128:(s + 1) * 128], kdm[0:NS, s, :], v_, s == 0, s == 3) for s in range(4)]
                self.mm_group(items, [kdm, v_], [U], skip=True)
                self.stt(Snew.rearrange("p a b -> p (a b)"), Sin32.rearrange("p a b -> p (a b)"), float(self.gam[h] ** 16),
                         U, ALU.mult, ALU.add)
                self.store(self.ss[:, h, :, :].rearrange("s d e -> d s e"), Snew, self._snsem)
                self.epilogue(Os[:, 0:NS], NS, sgb[4], gr[:, h:h + 1], self.yT[:, NH + h, T:T + NS], sq, rr, tfs,
                              defer=deferred, ms_ids=(6,), defer_mm=True)

        nst = len(steps)
        deferred = []
        for k in range(nst + 3):
            if 0 <= k - 2 < nst:
                S2(k - 2)
            if k < nst:
                S1(k)
                if k == nst - 1 and not with_sample:
                    self.issue_wout_load()
            if deferred and getattr(deferred[0], "is_a2", False):
                deferred.pop(0)()
            if 0 <= k - 1 < nst:
                S1c(k - 1)
            if 0 <= k - 2 < nst:
                S2b(k - 2)
            if 0 <= k - 3 < nst:
                S3(k - 3)
            if k < nst:
                S1b(k)
            while deferred and not getattr(deferred[0], "is_a2", False):
                deferred.pop(0)()
        while deferred:
            deferred.pop(0)()
        self.barrier()
        ar.release(m)
        return Wh.get(NH)

    def phase_s(self):
        ar = self.ar
        m = ar.mark()
        hreg = self.hT.ap.rearrange("p a b -> p (a b)")

        def carve(off, n):
            return hreg[:, off:off + n]

        KC = [TT(carve(i * 8192, 8192).rearrange("p (s j d) -> p s j d", s=4, j=NTT), Trk()) for i in range(2)]
        VC = [TT(carve(16384 + i * 8192, 8192).rearrange("p (s j d) -> p s j d", s=4, j=NTT), Trk()) for i in range(2)]
        KT = ar.alloc([4, T], BF16)
        e_b = [ar.alloc([512], F32) for _ in range(2)]
        sp_all = ar.alloc([NTT, NS], BF16)
        R_all = ar.alloc([NTT, NS], BF16)
        a_all = ar.alloc([NTT, NS], BF16)
        e_n = ar.alloc([NS], F32)
        sp_n = ar.alloc([NS], BF16)
        a_n = ar.alloc([NS], BF16)
        sq = ar.alloc([512], BF16)
        rr = ar.alloc([512], F32)
        tf = ar.alloc([512], F32)
        if not hasattr(self, "_kcsem"):
            self._kcsem = [self.dsem(f"kc{i}") for i in range(2)]
            self._vcsem = [self.dsem(f"vc{i}") for i in range(2)]
        gsb = self.gains[:, NDC:NDC + NH]

        def load(h):
            kc = KC[h % 2]
            vc = VC[h % 2]
            self.pool.deps([], [kc, vc])
            for s in range(4):
                self.pool.h.dma_start(out=kc.ap[:, s], in_=self.ck[s, :, h * 128:(h + 1) * 128].rearrange("(j p) d -> p j d", p=128)
                                      ).then_inc(self._kcsem[h % 2].h, 16)
                self._kcsem[h % 2].val += 16
            for s in range(4):
                self.pool.h.dma_start(out=vc.ap[:, s], in_=self.cv[s, :, h * 128:(h + 1) * 128].rearrange("(j p) d -> p j d", p=128)
                                      ).then_inc(self._vcsem[h % 2].h, 16)
                self._vcsem[h % 2].val += 16
            self.pool.done(Tok(self._kcsem[h % 2], self._kcsem[h % 2].val), [], [kc])
            self.pool.done(Tok(self._vcsem[h % 2], self._vcsem[h % 2].val), [], [vc])

        self.memset(self.dve, R_all, 0.0)
        load(0)
        for h in range(NH):
            if h + 1 < NH:
                load(h + 1)
            kc = KC[h % 2]
            vc = VC[h % 2]
            qs = self.sq_q[:, h, :]
            ks_ = self.sq_k[:, h, :]
            O = self.bank("sO", [3])
            self.memset(self.dve, O[:, 0:NS], 0.0)
            Zn = self.bank("sZn", [6])
            self.mm(Zn[0:NS, 0:NS], ks_, qs, skip=True)
            self.actf(e_n[0:NS], Zn[0:NS, 0:NS], AF.Exp)
            self.actf(sp_n[0:NS], e_n[0:NS], AF.Ln, bias=1.0)
            self.tt(self.dve, R_all[0:NS, NTT - 1, :], sp_n[0:NS], self.smask, ALU.mult)
            self.mm(Zn[0:NS, 0:NS], self.sntri, R_all[0:NS, NTT - 1, :], start=False, stop=True, skip=True)
            self.actf(a_n[0:NS], Zn[0:NS, 0:NS], AF.Exp)
            self.tt(self.dve, a_n[0:NS], a_n[0:NS], self.smask, ALU.mult)
            self.mm(O[:, 0:NS], self.sq_v[0:NS, h, :], a_n[0:NS], start=False, stop=False, skip=True)
            for s in range(4):
                for g in range(2):
                    bk = self.bank("sT", [0, 1, 2]).bitcast(BF16)
                    for k in range(8):
                        j = g * 8 + k
                        self.transpose(bk[:, k * 128:(k + 1) * 128], kc[:, s, j, :], self.ident, signal=(k == 7))
                    eng = self.dve if (s * 2 + g) % 2 == 0 else self.act
                    self.copy(eng, KT[:, s, g * 1024:(g + 1) * 1024], bk[:, 0:1024])
            Zb = {}
            for half in (1, 0):
                Z = self.bank("sZ", [4, 5])
                Zb[half] = Z
                items = []
                for jl in range(8):
                    j = half * 8 + jl
                    for s in range(4):
                        items.append((Z[:, jl * 64 + s * 16:jl * 64 + (s + 1) * 16], KT[:, s, j * 128:(j + 1) * 128],
                                      qs[:, s * 16:(s + 1) * 16], len(items) == 0, False))
                self.mm_group(items, [KT, self.sq_q], [Z], skip=True)
                e = e_b[half]
                spv = sp_all[:, half * 8:half * 8 + 8, :].rearrange("p a b -> p (a b)")
                self.actf(e, Z, AF.Exp)
                self.actf(spv, e, AF.Ln, bias=1.0)
                for j in range(half * 8 + 7, half * 8 - 1, -1):
                    if j == NTT - 1:
                        continue
                    self.tt(self.dve, R_all[:, j, :], R_all[:, j + 1, :], sp_all[:, j + 1, :], ALU.add)
                Rv = R_all[:, half * 8:half * 8 + 8, :].rearrange("p a b -> p (a b)")
                items = [(Z, self.ntri, spv, False, False), (Z, self.nones, Rv, False, True)]
                self.mm_group(items, [self.cb, sp_all, R_all], [Z], skip=True)
                av = a_all[:, half * 8:half * 8 + 8, :].rearrange("p a b -> p (a b)")
                self.actf(av, Z, AF.Exp)
                items = []
                for jl in range(8):
                    j = half * 8 + jl
                    for s in range(4):
                        items.append((O[:, s * 16:(s + 1) * 16], vc[:, s, j, :], a_all[:, j, s * 16:(s + 1) * 16], False,
                                      (half == 0 and jl == 7 and s == 3)))
                self.mm_group(items, [vc, a_all], [O], skip=True)
            self.epilogue(O[:, 0:NS], NS, self.sq_g[:, h, :], gsb[:, h:h + 1], self.yT[:, h, T:T + NS], sq, rr, tf)
        self.barrier()
        ar.release(m)

    def phase_e(self, seq, with_sample):
        ar = self.ar
        m = ar.mark()
        if not hasattr(self, "_gpsem"):
            self._gpsem = self.dsem("gp")
            self._xesem = [self.dsem(f"xe{i}") for i in range(2)]
            self._yosem = [self.dsem(f"yo{i}") for i in range(2)]
        if not self._wo_loaded:
            self.issue_wout_load()
        self._wo_loaded = False
        Wo = self.wout_view()
        gp = ar.alloc([D], F32)
        self.sp.dma(gp.ap, self.gpost_d, self._gpsem, [], [gp])
        xs_ = [ar.alloc([D], F32) for _ in range(2)]
        tfs = [ar.alloc([512], F32) for _ in range(2)]
        junk = ar.alloc([512], BF16)
        stat = ar.alloc([16], F32)
        ntiles = NTT + (1 if with_sample else 0)

        def load(tt_):
            x_t = xs_[tt_ % 2]
            if tt_ < NTT:
                self.sp.dma(x_t.ap, self.xp[seq, tt_ * 128:(tt_ + 1) * 128, :], self._xesem[tt_ % 2], [], [x_t])
            else:
                self.sp.dma(x_t.ap[0:NS, :], self.xs[:, :], self._xesem[tt_ % 2], [], [x_t])

        load(0)
        for tt_ in range(ntiles):
            if tt_ + 1 < ntiles:
                load(tt_ + 1)
            n = 128 if tt_ < NTT else NS
            t0 = tt_ * 128
            x_t = xs_[tt_ % 2]
            bks = []
            st = stat[0:n, (tt_ % 2) * 8:(tt_ % 2) * 8 + 8]
            for c in range(4):
                bk = self.bank("e", [0, 1, 2, 3, 4, 5, 6, 7])
                items = [(bk[0:n, :], self.yT[:, mc, t0:t0 + n], Wo[:, mc, c * 512:(c + 1) * 512], mc == 0, mc == NDC - 1)
                         for mc in range(NDC)]
                self.mm_group(items, [self.yT, Wo], [bk])
                self.actf(junk[0:n], bk[0:n, :], AF.Square, accum=st[:, c:c + 1])
                bks.append(bk)
            self.tt(self.dve, st[:, 4:5], st[:, 0:1], st[:, 1:2], ALU.add)
            self.tt(self.dve, st[:, 5:6], st[:, 2:3], st[:, 3:4], ALU.add)
            self.tt(self.dve, st[:, 4:5], st[:, 4:5], st[:, 5:6], ALU.add)
            self.actf(st[:, 6:7], st[:, 4:5], AF.Ln, scale=1.0 / D, bias=EPS)
            self.actf(st[:, 7:8], st[:, 6:7], AF.Exp, scale=-0.5)
            for c in range(4):
                tfc = tfs[c % 2]
                self.stt(tfc[0:n], bks[c][0:n, :], st[:, 7:8], gp[0:n, c * 512:(c + 1) * 512], ALU.mult, ALU.mult)
                self.tt(self.pool, x_t[0:n, c * 512:(c + 1) * 512], x_t[0:n, c * 512:(c + 1) * 512], tfc[0:n], ALU.add)
            if tt_ < NTT:
                self.store(self.yp[seq, t0:t0 + 128, :], x_t, self._yosem[tt_ % 2])
            else:
                self.store(self.ys[:, :], x_t[0:NS], self._yosem[tt_ % 2])
        self.barrier()
        ar.release(m)

    def build(self):
        with self.nc.Block():
            self.setup()
            gi = 0
            import os as _os
            stop = _os.environ.get("KDEV_STOP", "")
            seqs = [int(c) for c in _os.environ.get("KDEV_SEQS", "01")]
            self._wpre = {}
            for seq in seqs:
                ws = (seq == 1)
                if stop == "setup":
                    continue
                if gi not in self._wpre:
                    self._wpre[gi] = self.load_wgroup(gi, self._sb_cols(0))
                self.phase_a(seq, ws)
                if stop == "a":
                    continue
                W = self.phase_c(seq, ws, gi)
                gi += NH
                if stop == "c":
                    continue
                W = self.phase_d(seq, ws, gi, W, None)
                gi += NH
                if stop == "d":
                    continue
                if ws:
                    self.phase_s()
                if stop == "s":
                    continue
                if seq == seqs[0] and len(seqs) > 1:
                    self._wpre[gi] = self.load_wgroup(gi, self._sb_cols(0))
                self.phase_e(seq, ws)
            for tok in self.out_toks:
                self.sp.wait(tok)
            self.barrier()
        return self.nc


_CACHE = {}


def _get_prog():
    if "nc" not in _CACHE:
        b = Builder()
        _CACHE["nc"] = b.build()
    return _CACHE["nc"]


def make_in_maps(x_prompt, x_sample, cache_sb_k, cache_sb_v, state_ret, norm_pre, w_in, sb_head_norm,
                 ret_head_norm, w_out, norm_post, cores):
    cb, cf, rope, _ = _consts()
    f = lambda a: np.ascontiguousarray(a, dtype=np.float32)
    g1T = f(norm_pre[0].reshape(NDC, 128).T)
    gsb = f(sb_head_norm[0].reshape(NH, 128).T)
    gr = f(ret_head_norm[0].reshape(NH, 128).T)
    gpost = f(np.broadcast_to(norm_post[0][None, :], (128, D)))
    w_in0 = f(w_in[0])
    w_out0 = f(w_out[0])
    maps = []
    for c in cores:
        maps.append({
            "xp": f(x_prompt[2 * c:2 * c + 2]),
            "xs": f(x_sample[4 * c:4 * c + 4].reshape(NS, D)),
            "ck": f(cache_sb_k[0, 4 * c:4 * c + 4].reshape(4, T, NH * HD)),
            "cv": f(cache_sb_v[0, 4 * c:4 * c + 4].reshape(4, T, NH * HD)),
            "sr": f(state_ret[0, 4 * c:4 * c + 4]),
            "w_in": w_in0, "w_out": w_out0, "g1T": g1T, "gsb": gsb, "gr": gr, "gpost": gpost,
            "cb": cb, "cf": cf, "rope": rope,
        })
    return maps


def assemble(results, ncores):
    yp = np.concatenate([r["yp"] for r in results], axis=0)
    ys = np.concatenate([r["ys"].reshape(4, 16, D) for r in results], axis=0)
    kp = np.concatenate([r["kp"].reshape(2, T, NH, HD) for r in results], axis=0)[None]
    vp = np.concatenate([r["vp"].reshape(2, T, NH, HD) for r in results], axis=0)[None]
    spo = np.concatenate([r["spo"] for r in results], axis=0)[None]
    ks = np.concatenate([r["ks"].reshape(4, 16, NH, HD) for r in results], axis=0)[None]
    vs = np.concatenate([r["vs"].reshape(4, 16, NH, HD) for r in results], axis=0)[None]
    ss = np.concatenate([r["ss"] for r in results], axis=0)[None]
    return (yp, ys, kp, vp, spo, ks, vs, ss)


def kernel(x_prompt, x_sample, cache_sb_k, cache_sb_v, state_ret, norm_pre, w_in, sb_head_norm,
           ret_head_norm, w_out, norm_post):
    args = [np.asarray(a) for a in (x_prompt, x_sample, cache_sb_k, cache_sb_v, state_ret, norm_pre, w_in,
                                    sb_head_norm, ret_head_norm, w_out, norm_post)]
    nc = _get_prog()
    cores = list(range(8))
    maps = make_in_maps(*args, cores)
    res = run_bass_kernel_spmd(nc, maps, core_ids=cores)
    outs = assemble(res.results, 8)
    return tuple(np.ascontiguousarray(o, dtype=np.float32) for o in outs)
```
